# Optimizing a Trainium2 kernel written in Bass

```python
import math
import jax, jax.numpy as jnp
from jax import lax
import numpy as np

D_MODEL = 1024
BATCH = 8
SEQ = 4096
DEPTH = 1

HEAD_DIM = 64
D_MIX = D_MODEL
D_MIX_A = D_MIX // 2
D_MIX_B = D_MIX - D_MIX_A
N_HEADS_A = D_MIX_A // HEAD_DIM
DIFF_QK_DIM = HEAD_DIM // 2
N_HEADS_B = D_MIX_B // HEAD_DIM
GQA_GROUP = 4
N_KV_B = N_HEADS_B // GQA_GROUP
D_FF = ((8 * D_MODEL // 3 + 127) // 128) * 128
GRID_W = 64
ROPE_THETA = 10000.0
ROPE_AXIS_DIM = HEAD_DIM // 2
N_BUCKETS = 32
MAX_DISTANCE = 128
Q_BLOCK = 128
EPS = 1e-6
COLS_QA = N_HEADS_A * 2 * DIFF_QK_DIM
COLS_KA = N_HEADS_A * 2 * DIFF_QK_DIM
COLS_VA = N_HEADS_A * HEAD_DIM
COLS_QB = N_HEADS_B * HEAD_DIM
COLS_KB = N_KV_B * HEAD_DIM
COLS_VB = N_KV_B * HEAD_DIM
D_IN_PROJ = COLS_QA + COLS_KA + COLS_VA + COLS_QB + COLS_KB + COLS_VB

kernel_name = "hybrid_diffattn_gqa_axialrope_macaron"


def rms_norm(x, g):
    xf = x.astype(jnp.float32)
    y = xf * lax.rsqrt(jnp.mean(xf * xf, axis=-1, keepdims=True) + EPS)
    return (y * g.astype(jnp.float32)).astype(x.dtype)


def swiglu(x, w_gate, w_up, w_down):
    return (jax.nn.silu(x @ w_gate) * (x @ w_up)) @ w_down


def t5_bucket(rel):
    nb = N_BUCKETS // 2
    max_exact = nb // 2
    ret = jnp.where(rel > 0, nb, 0)
    n = jnp.abs(rel)
    nf = jnp.maximum(n, 1).astype(jnp.float32)
    large = max_exact + (jnp.log(nf / max_exact) / math.log(MAX_DISTANCE / max_exact)
                         * (nb - max_exact)).astype(jnp.int32)
    large = jnp.minimum(large, nb - 1)
    return ret + jnp.where(n < max_exact, n, large)


def axial_rope_tables(S):
    rows = S // GRID_W
    row = jnp.repeat(jnp.arange(rows, dtype=jnp.int32), GRID_W).astype(jnp.float32)
    col = jnp.tile(jnp.arange(GRID_W, dtype=jnp.int32), rows).astype(jnp.float32)
    inv = ROPE_THETA ** (-jnp.arange(0, ROPE_AXIS_DIM, 2, dtype=jnp.float32) / ROPE_AXIS_DIM)
    ang = jnp.concatenate([row[:, None] * inv[None], col[:, None] * inv[None]], axis=-1)
    return jnp.cos(ang), jnp.sin(ang)


def apply_rope(x, cos, sin):
    xf = x.astype(jnp.float32).reshape(x.shape[:-1] + (HEAD_DIM // 2, 2))
    x0, x1 = xf[..., 0], xf[..., 1]
    out = jnp.stack([x0 * cos - x1 * sin, x0 * sin + x1 * cos], axis=-1)
    return out.reshape(x.shape).astype(x.dtype)


def split_blocks(t):
    B, H, S, d = t.shape
    return t.reshape(B, H, S // Q_BLOCK, Q_BLOCK, d).transpose(2, 0, 1, 3, 4)


def merge_blocks(t):
    nblk, B, H, qb, d = t.shape
    return t.transpose(1, 2, 0, 3, 4).reshape(B, H, nblk * qb, d)


def diff_attention(q1, q2, k1, k2, v, lam, rel_bias):
    S = q1.shape[2]
    nblk = S // Q_BLOCK
    scale = DIFF_QK_DIM ** -0.5
    kpos = jnp.arange(S, dtype=jnp.int32)
    table = rel_bias.astype(jnp.float32)

    def block(args):
        i, a1, a2 = args
        qpos = i * Q_BLOCK + jnp.arange(Q_BLOCK, dtype=jnp.int32)
        bias = table[t5_bucket(kpos[None, :] - qpos[:, None])]
        bias = bias.transpose(2, 0, 1)[None]
        s1 = jnp.einsum('bhqd,bhkd->bhqk', a1, k1, preferred_element_type=jnp.float32) * scale + bias
        s2 = jnp.einsum('bhqd,bhkd->bhqk', a2, k2, preferred_element_type=jnp.float32) * scale + bias
        p = jax.nn.softmax(s1, axis=-1) - lam * jax.nn.softmax(s2, axis=-1)
        return jnp.einsum('bhqk,bhkd->bhqd', p.astype(v.dtype), v)

    out = lax.map(block, (jnp.arange(nblk, dtype=jnp.int32), split_blocks(q1), split_blocks(q2)))
    return merge_blocks(out)


def gqa_attention(q, k, v):
    B, HB, S, d = q.shape
    scale = d ** -0.5

    def block(qb):
        qg = qb.reshape(B, N_KV_B, GQA_GROUP, Q_BLOCK, d)
        s = jnp.einsum('bkgqd,bksd->bkgqs', qg, k, preferred_element_type=jnp.float32) * scale
        p = jax.nn.softmax(s, axis=-1)
        o = jnp.einsum('bkgqs,bksd->bkgqd', p.astype(v.dtype), v)
        return o.reshape(B, HB, Q_BLOCK, d)

    return merge_blocks(lax.map(block, split_blocks(q)))


def setup_inputs(seed: int = 0) -> dict:
    key = jax.random.key(seed)
    ks = jax.random.split(key, 24)
    f32 = jnp.float32

    def w(k, shape, fan_in):
        return jax.random.normal(k, shape, f32) * fan_in ** -0.5

    def gain(k, shape):
        return 1.0 + 0.01 * jax.random.normal(k, shape, f32)

    L = DEPTH
    return {
        "x": jax.random.normal(ks[0], (BATCH, SEQ, D_MODEL), f32),
        "ffn1_norm": gain(ks[1], (L, D_MODEL)),
        "ffn1_w_gate": w(ks[2], (L, D_MODEL, D_FF), D_MODEL),
        "ffn1_w_up": w(ks[3], (L, D_MODEL, D_FF), D_MODEL),
        "ffn1_w_down": w(ks[4], (L, D_FF, D_MODEL), D_FF),
        "mix_norm": gain(ks[5], (L, D_MODEL)),
        "w_in": w(ks[6], (L, D_MODEL, D_IN_PROJ), D_MODEL),
        "lambda_q1": 0.1 * jax.random.normal(ks[7], (L, DIFF_QK_DIM), f32),
        "lambda_k1": 0.1 * jax.random.normal(ks[8], (L, DIFF_QK_DIM), f32),
        "lambda_q2": 0.1 * jax.random.normal(ks[9], (L, DIFF_QK_DIM), f32),
        "lambda_k2": 0.1 * jax.random.normal(ks[10], (L, DIFF_QK_DIM), f32),
        "diff_subln": gain(ks[11], (L, HEAD_DIM)),
        "q_norm": gain(ks[12], (L, HEAD_DIM)),
        "k_norm": gain(ks[13], (L, HEAD_DIM)),
        "rel_bias": 0.5 * jax.random.normal(ks[14], (N_BUCKETS, N_HEADS_A), f32),
        "w_out": w(ks[15], (L, D_MIX, D_MODEL), D_MIX),
        "ffn2_norm": gain(ks[16], (L, D_MODEL)),
        "ffn2_w_gate": w(ks[17], (L, D_MODEL, D_FF), D_MODEL),
        "ffn2_w_up": w(ks[18], (L, D_MODEL, D_FF), D_MODEL),
        "ffn2_w_down": w(ks[19], (L, D_FF, D_MODEL), D_FF),
        "final_norm": gain(ks[20], (D_MODEL,)),
    }


def reference(x, ffn1_norm, ffn1_w_gate, ffn1_w_up, ffn1_w_down, mix_norm, w_in,
              lambda_q1, lambda_k1, lambda_q2, lambda_k2, diff_subln, q_norm, k_norm,
              rel_bias, w_out, ffn2_norm, ffn2_w_gate, ffn2_w_up, ffn2_w_down, final_norm):
    B, S, _ = x.shape
    cos, sin = axial_rope_tables(S)
    offs = np.cumsum([0, COLS_QA, COLS_KA, COLS_VA, COLS_QB, COLS_KB, COLS_VB])

    for l in range(DEPTH):
        lambda_init = 0.8 - 0.6 * math.exp(-0.3 * l)

        x = x + 0.5 * swiglu(rms_norm(x, ffn1_norm[l]), ffn1_w_gate[l], ffn1_w_up[l], ffn1_w_down[l])

        h = rms_norm(x, mix_norm[l])
        proj = h @ w_in[l]
        qa, ka, va, qb, kb, vb = [proj[..., offs[i]:offs[i + 1]] for i in range(6)]

        qa = qa.reshape(B, S, N_HEADS_A, 2, DIFF_QK_DIM).transpose(0, 2, 3, 1, 4)
        ka = ka.reshape(B, S, N_HEADS_A, 2, DIFF_QK_DIM).transpose(0, 2, 3, 1, 4)
        va = va.reshape(B, S, N_HEADS_A, HEAD_DIM).transpose(0, 2, 1, 3)
        lam = (jnp.exp(jnp.sum(lambda_q1[l].astype(jnp.float32) * lambda_k1[l].astype(jnp.float32)))
               - jnp.exp(jnp.sum(lambda_q2[l].astype(jnp.float32) * lambda_k2[l].astype(jnp.float32)))
               + lambda_init)
        oa = diff_attention(qa[:, :, 0], qa[:, :, 1], ka[:, :, 0], ka[:, :, 1], va, lam, rel_bias)
        oa = rms_norm(oa, diff_subln[l]) * (1.0 - lambda_init)
        oa = oa.transpose(0, 2, 1, 3).reshape(B, S, D_MIX_A)

        qb = rms_norm(qb.reshape(B, S, N_HEADS_B, HEAD_DIM), q_norm[l]).transpose(0, 2, 1, 3)
        kb = rms_norm(kb.reshape(B, S, N_KV_B, HEAD_DIM), k_norm[l]).transpose(0, 2, 1, 3)
        vb = vb.reshape(B, S, N_KV_B, HEAD_DIM).transpose(0, 2, 1, 3)
        qb = apply_rope(qb, cos, sin)
        kb = apply_rope(kb, cos, sin)
        ob = gqa_attention(qb, kb, vb).transpose(0, 2, 1, 3).reshape(B, S, D_MIX_B)

        x = x + jnp.concatenate([oa, ob], axis=-1) @ w_out[l]

        x = x + 0.5 * swiglu(rms_norm(x, ffn2_norm[l]), ffn2_w_gate[l], ffn2_w_up[l], ffn2_w_down[l])

    return rms_norm(x, final_norm)
```

```python
import math
from contextlib import ExitStack

import numpy as np
import ml_dtypes

import concourse.bass as bass
import concourse.mybir as mybir
from concourse.bass_utils import run_bass_kernel_spmd

F32 = mybir.dt.float32
BF16 = mybir.dt.bfloat16
AF = mybir.ActivationFunctionType
ALU = mybir.AluOpType
AX = mybir.AxisListType

D = 1024
FF = 2816
NFC = FF // 128
DIN = 2304
EPS = 1e-6
SEQ = 4096
N_CORES = 8
LAMBDA_INIT = 0.8 - 0.6 * math.exp(-0.3 * 0)
SC_A = 32 ** -0.5
import os as _os0
N_DUMMY = int(_os0.environ.get('K_NDUMMY', '1'))
DUMMY_W = int(_os0.environ.get('K_DW', '320'))
FAST_RECIP = int(_os0.environ.get('K_FR', '0'))
ZW = 1152
VL = 1280


class Eng:
    def __init__(self, nc, h, name, es):
        self.h = h
        self.name = name
        self.sem = es.enter_context(nc.semaphore(name + "_sem"))
        self.n = 0
        self.seen = {}

    def wait(self, *toks):
        for t in toks:
            if t is None:
                continue
            if isinstance(t, (list,)):
                self.wait(*t)
                continue
            key, sem, val = t
            if self.seen.get(key, 0) >= val:
                continue
            self.h.wait_ge(sem, val)
            self.seen[key] = val

    def tok(self, ins):
        self.n += 1
        ins.then_inc(self.sem, 1)
        return (self.name, self.sem, self.n)


class DSem:
    def __init__(self, nc, name, es):
        self.name = name
        self.sem = es.enter_context(nc.semaphore(name))
        self.count = 0
        self.last = None
        _ALL_DSEMS.append(self)

    def issue(self, q, pairs, waits=()):
        q.wait(*waits)
        q.wait(self.last)
        if q.name == "pool":
            if len(_POOL_OUT) >= 2:
                q.wait(_POOL_OUT[-2])
        for (o, i) in pairs:
            q.h.dma_start(out=o, in_=i).then_inc(self.sem, 16)
            self.count += 16
        self.last = (self.name, self.sem, self.count)
        if q.name == "pool":
            _POOL_OUT.append(self.last)
        return self.last


class K:
    pass


class _Stop(Exception):
    pass


_ALL_DSEMS = []
_POOL_OUT = []
_STOP_AT = [None]


_CK_COUNT = {}


def ck(name):
    if _STOP_AT[0] is None:
        return
    tgt = _STOP_AT[0]
    n = 1
    if "#" in tgt:
        tgt, n = tgt.split("#")
        n = int(n)
    if tgt == name:
        _CK_COUNT[name] = _CK_COUNT.get(name, 0) + 1
        if _CK_COUNT[name] == n:
            raise _Stop(name)


def build_program(S=SEQ, debug=False, phases=(1, 2, 3)):
    NT = S // 128
    NCH = S // 512
    nc = bass.Bass("TRN2", target_bir_lowering=False)
    es = ExitStack()
    del _POOL_OUT[:]

    def din(name, shape, dt=F32):
        return nc.dram_tensor(name, list(shape), dt, kind="ExternalInput").ap()

    def dscr(name, shape, dt):
        kind = "ExternalOutput" if debug else "Internal"
        return nc.dram_tensor(name, list(shape), dt, kind=kind).ap()

    x = din("x", [S, D])
    w_g = [din("ffn1_w_gate", [D, FF]), din("ffn2_w_gate", [D, FF])]
    w_u = [din("ffn1_w_up", [D, FF]), din("ffn2_w_up", [D, FF])]
    w_d = [din("ffn1_w_down", [FF, D]), din("ffn2_w_down", [FF, D])]
    g_ffn = [din("ffn1_norm", [1, D]), din("ffn2_norm", [1, D])]
    g_mix = din("mix_norm", [1, D])
    g_fin = din("final_norm", [1, D])
    w_in = din("w_in", [D, DIN])
    w_out = din("w_out", [D, D])
    lam_in = [din(n, [1, 32]) for n in ("lambda_q1", "lambda_k1", "lambda_q2", "lambda_k2")]
    g_sub = din("diff_subln", [1, 64])
    g_q = din("q_norm", [1, 64])
    g_k = din("k_norm", [1, 64])
    rel_bias = din("rel_bias", [32, 8])
    ident_d = din("ident", [128, 128], BF16)
    ohm_d = din("ohm", [32, VL])
    jmat_d = din("jmat", [128, 128])
    c2_d = din("c2", [S, 64])
    s2_d = din("s2", [S, 64])
    out = nc.dram_tensor("out", [S, D], F32, kind="ExternalOutput").ap()

    wgb = [dscr("wg1b", [11, 128, 2048], BF16), dscr("wg2b", [11, 128, 2048], BF16)]
    wub = [dscr("wu1b", [11, 128, 2048], BF16), dscr("wu2b", [11, 128, 2048], BF16)]
    wd2b = dscr("wd2b", [FF, D], BF16)
    winb = dscr("winb", [9, 128, 2048], BF16)
    woutb = dscr("woutb", [D, D], BF16)
    x1s = dscr("x1s", [S, D], F32)
    qaT_d = dscr("qaT", [4, 128, S], BF16)
    kaT_d = dscr("kaT", [4, 128, S], BF16)
    va_d = dscr("va", [S, 520], BF16)
    qbT_d = dscr("qbT", [4, 128, S], BF16)
    kbT_d = dscr("kbT", [128, S], BF16)
    vb_d = dscr("vb", [S, 130], BF16)
    oT_d = dscr("oT", [8, 128, S], BF16)
    vbias_d = dscr("vbias", [8, VL], F32)
    lrow_d = nc.dram_tensor("lrow", [2, 1024], F32, kind="Internal").ap()
    rrow_d = nc.dram_tensor("rrow", [2, 1024], F32, kind="Internal").ap()
    arow_d = nc.dram_tensor("arow", [2, 512], F32, kind="Internal").ap()

    pe = Eng(nc, nc.tensor, "pe", es)
    act = Eng(nc, nc.scalar, "act", es)
    dve = Eng(nc, nc.vector, "dve", es)
    pool = Eng(nc, nc.gpsimd, "pool", es)
    sp = Eng(nc, nc.sync, "sp", es)

    def sb(name, shape, dt):
        return es.enter_context(nc.sbuf_tensor("g_" + name, list(shape), dt))

    ident = sb("ident", [128, 128], BF16)
    stats = sb("stats", [128, 2048], F32)
    stat_ptr = [0]

    def stat_cols(n):
        a = stat_ptr[0]
        stat_ptr[0] += n
        assert stat_ptr[0] <= 2048, "stats overflow"
        return stats[:, a:a + n]

    junk = sb("junk", [128, 1024], BF16)
    epsc = sb("epsc", [128, 1], F32)
    neglam = sb("neglam", [128, 1], F32)
    gsubc = sb("gsubc", [64, 1], F32)
    gq_t = sb("gq_t", [128, 64], F32)
    gk_t = sb("gk_t", [128, 64], F32)
    bconst = sb("bconst", [128, 16], F32)
    lamv = sb("lamv", [128, 4, 32], F32)
    lamt = sb("lamt", [128, 2, 32], F32)
    lams = sb("lams", [128, 4], F32)

    ds_const = DSem(nc, "ds_const", es)
    ds_cast = [DSem(nc, f"ds_cast{i}", es) for i in range(8)]

    def bcast_rows(src, n_part, n):
        return bass.AP(src.tensor, src.offset, [[0, n_part], [1, n]])

    tk_const = ds_const.issue(sp, [
        (ident[:], ident_d),
        (gq_t[:], bcast_rows(g_q, 128, 64)),
        (gk_t[:], bcast_rows(g_k, 128, 64)),
        (bconst[:, 0:8], bcast_rows(rel_bias[15:16, :], 128, 8)),
        (bconst[:, 8:16], bcast_rows(rel_bias[31:32, :], 128, 8)),
        (lamv[:, 0, :], bcast_rows(lam_in[0], 128, 32)),
        (lamv[:, 1, :], bcast_rows(lam_in[1], 128, 32)),
        (lamv[:, 2, :], bcast_rows(lam_in[2], 128, 32)),
        (lamv[:, 3, :], bcast_rows(lam_in[3], 128, 32)),
        (gsubc[:], bass.AP(g_sub.tensor, g_sub.offset, [[1, 64], [1, 1]])),
    ])
    tk_eps = dve.tok(dve.h.memset(epsc[:], EPS))
    dve.wait(tk_const)
    t1 = dve.tok(dve.h.tensor_tensor(out=lamt[:, 0, :], in0=lamv[:, 0, :], in1=lamv[:, 1, :], op=ALU.mult))
    t2 = dve.tok(dve.h.tensor_tensor(out=lamt[:, 1, :], in0=lamv[:, 2, :], in1=lamv[:, 3, :], op=ALU.mult))
    dve.wait(t1, t2)
    t3 = dve.tok(dve.h.tensor_reduce(out=lams[:, 0:2], in_=lamt[:], axis=AX.X, op=ALU.add))
    act.wait(t3)
    t4 = act.tok(act.h.activation(out=lams[:, 2:4], in_=lams[:, 0:2], func=AF.Exp))
    dve.wait(t4)
    t5 = dve.tok(dve.h.tensor_tensor(out=neglam[:], in0=lams[:, 3:4], in1=lams[:, 2:3], op=ALU.subtract))
    dve.wait(t5)
    t6 = dve.tok(dve.h.tensor_scalar(out=neglam[:], in0=neglam[:], scalar1=-LAMBDA_INIT, scalar2=None, op0=ALU.add))
    t7 = dve.tok(dve.h.tensor_scalar(out=gsubc[:], in0=gsubc[:], scalar1=(1.0 - LAMBDA_INIT), scalar2=None, op0=ALU.mult))
    t8 = dve.tok(dve.h.tensor_scalar(out=gq_t[:], in0=gq_t[:], scalar1=0.125, scalar2=None, op0=ALU.mult))
    bdiff = sb("bdiff", [128, 8], F32)
    cneg = sb("cneg", [128, 8], F32)
    t9 = dve.tok(dve.h.tensor_tensor(out=bdiff[:], in0=bconst[:, 8:16], in1=bconst[:, 0:8], op=ALU.subtract))
    act.wait(tk_const)
    t10 = act.tok(act.h.activation(out=cneg[:], in_=bconst[:, 0:8], func=AF.Exp, scale=-1.0))
    tk_consts_ready = [tk_const, tk_eps, t6, t7, t8, t9, t10]

    import os as _os
    _skip = _os.environ.get("KSKIP", "")

    def cast_dram(dsem, dst, src, ncols):
        if "cast" in _skip:
            return None
        pairs = []
        c0 = 0
        while c0 < ncols:
            c1 = min(ncols, c0 + 1408)
            pairs.append((dst[:, c0:c1], src[:, c0:c1]))
            c0 = c1
        return dsem.issue(pool, pairs)

    p0s = ExitStack()
    rb_sb = p0s.enter_context(nc.sbuf_tensor("z_rb", [32, 8], F32))
    ohm = p0s.enter_context(nc.sbuf_tensor("z_ohm", [32, VL], F32))
    vb_sb = p0s.enter_context(nc.sbuf_tensor("z_vb", [8, VL], F32))
    zps = p0s.enter_context(nc.psum_tensor("z_ps", [128, 1536], F32))
    ds_z0 = DSem(nc, "ds_z0", es)
    tk_z0 = ds_z0.issue(sp, [(rb_sb[:], rel_bias), (ohm[:], ohm_d)])
    pe.wait(tk_z0)
    for (a_, b_) in ((0, 512), (512, 1024), (1024, 1280)):
        ins = pe.h.matmul(zps[0:8, a_:b_], lhsT=rb_sb[:], rhs=ohm[:, a_:b_], start=True, stop=True)
    tz = pe.tok(ins)
    act.wait(tz)
    z1 = act.tok(act.h.activation(out=vb_sb[:, :], in_=zps[0:8, 0:VL], func=AF.Exp))
    tk_vst = ds_z0.issue(sp, [(vbias_d, vb_sb[:])], waits=[z1])
    p0s.close()
    for e_ in (pe, act, dve, pool, sp):
        e_.wait(tk_vst)
    tk_wg = [[], None]
    tk_wu = [[], None]
    for g_ in range(11):
        cs = slice(g_ * 256, (g_ + 1) * 256)
        tk_wg[0].append(ds_cast[0].issue(pool, [(wgb[0][g_].rearrange("p (cc f) -> cc p f", f=256), w_g[0][:, cs].rearrange("(cc p) f -> cc p f", p=128))]))
        tk_wu[0].append(ds_cast[1].issue(pool, [(wub[0][g_].rearrange("p (cc f) -> cc p f", f=256), w_u[0][:, cs].rearrange("(cc p) f -> cc p f", p=128))]))

    def token_phase(which):
        pes = ExitStack()

        def psb(name, shape, dt):
            return pes.enter_context(nc.sbuf_tensor(f"p{which}_{name}", list(shape), dt))

        def pps(name, shape, dt):
            return pes.enter_context(nc.psum_tensor(f"p{which}_{name}", list(shape), dt))

        wd = psb("wd", [128, NFC, D], BF16)
        xt = [psb("xt0", [128, 4, D], F32), psb("xt1", [128, 4, D], F32)]
        hT = [psb("hT0", [128, 8, 512], BF16), psb("hT1", [128, 8, 512], BF16)]
        actT = psb("actT", [128, NFC, 512], BF16)
        ws = [psb(f"ws{i}", [128, 2, 8, 256], BF16) for i in range(3)]
        gA = psb("gA", [128, D], F32)
        gB = psb("gB", [128, D], F32)
        hn = [psb("hn0", [128, D], BF16), psb("hn1", [128, D], BF16)]
        sg = [psb("sg0", [128, 512], BF16), psb("sg1", [128, 512], BF16)]
        tp = [pps("tp0", [128, 1024], BF16), pps("tp1", [128, 1024], BF16)]
        pg = [pps("pg0", [128, 512], F32), pps("pg1", [128, 512], F32)]
        pu = [pps("pu0", [128, 512], F32), pps("pu1", [128, 512], F32)]
        py = [pps("py0", [128, 512], F32), pps("py1", [128, 512], F32)]

        ds_x = [DSem(nc, f"p{which}_dsx{i}", pes) for i in range(2)]
        ds_xp = [DSem(nc, f"p{which}_dsxp{i}", pes) for i in range(2)]
        ds_ws = [DSem(nc, f"p{which}_dsw{i}", pes) for i in range(3)]
        ds_g = DSem(nc, f"p{which}_dsg", pes)
        ds_st = [DSem(nc, f"p{which}_dst{i}", pes) for i in range(4)]

        if which == 0:
            c2t = [psb("c2t0", [128, 4, 64], F32), psb("c2t1", [128, 4, 64], F32)]
            s2t = [psb("s2t0", [128, 4, 64], F32), psb("s2t1", [128, 4, 64], F32)]
            va_sb = psb("va_sb", [128, 4, 8, 65], BF16)
            vb_sb = psb("vb_sb", [128, 4, 2, 65], BF16)
            qaT_sb = psb("qaT_sb", [128, 4, 512], BF16)
            kaT_sb = psb("kaT_sb", [128, 4, 512], BF16)
            qbT_sb = psb("qbT_sb", [128, 4, 512], BF16)
            kbT_sb = psb("kbT_sb", [128, 512], BF16)
            qkf = [psb("qkf0", [128, 10, 64], F32), psb("qkf1", [128, 10, 64], F32)]
            qsq = psb("qsq", [128, 10, 64], F32)
            qrt = psb("qrt", [128, 10, 64], F32)
            qtm = psb("qtm", [128, 10, 64], F32)
            qkn = [psb(f"qkn{i}", [128, 640], BF16) for i in range(4)]
            ds_aux = [DSem(nc, f"p{which}_dsr{i}", pes) for i in range(2)]
            ds_auxp = [DSem(nc, f"p{which}_dsrp{i}", pes) for i in range(2)]
        else:
            wout = psb("wout", [128, 8, D], BF16)
            ot = [psb("ot0", [128, 8, 512], BF16), psb("ot1", [128, 8, 512], BF16)]
            ds_aux = [DSem(nc, f"p{which}_dso{i}", pes) for i in range(2)]
            ds_auxp = [DSem(nc, f"p{which}_dsop{i}", pes) for i in range(2)]

        st = K()
        if which == 1:
            for e_ in (pe, act, dve, pool, sp):
                e_.wait(P2_done)
        gsrcA = g_ffn[which]
        gsrcB = g_mix if which == 0 else g_fin
        tk_g = ds_g.issue(sp, [(gA[:], bcast_rows(gsrcA, 128, D)), (gB[:], bcast_rows(gsrcB, 128, D))])
        if which == 0:
            tk_wd = None if "wd" in _skip else ds_cast[2].issue(pool, [(wd[:], w_d[0].rearrange("(fc p) d -> p fc d", p=128))])
            for g_ in range(9):
                cs_ = slice(g_ * 256, (g_ + 1) * 256)
                tk_win = ds_cast[3].issue(pool, [(winb[g_].rearrange("p (cc f) -> cc p f", f=256), w_in[:, cs_].rearrange("(cc p) f -> cc p f", p=128))])
            bg_casts = []
            for g_ in range(11):
                cs_ = slice(g_ * 256, (g_ + 1) * 256)
                bg_casts.append(lambda cs_=cs_, g_=g_: tk_wg.__setitem__(1, ds_cast[4].issue(pool, [(wgb[1][g_].rearrange("p (cc f) -> cc p f", f=256), w_g[1][:, cs_].rearrange("(cc p) f -> cc p f", p=128))])))
                bg_casts.append(lambda cs_=cs_, g_=g_: tk_wu.__setitem__(1, ds_cast[5].issue(pool, [(wub[1][g_].rearrange("p (cc f) -> cc p f", f=256), w_u[1][:, cs_].rearrange("(cc p) f -> cc p f", p=128))])))
            for r_ in range(4):
                rs_ = slice(r_ * 704, (r_ + 1) * 704)
                bg_casts.append(lambda rs_=rs_: setattr(st, "tk_wd2", ds_cast[6].issue(pool, [(wd2b[rs_, :], w_d[1][rs_, :])])))
            for r_ in range(2):
                rs_ = slice(r_ * 512, (r_ + 1) * 512)
                bg_casts.append(lambda rs_=rs_: setattr(st, "tk_wout", ds_cast[7].issue(pool, [(woutb[rs_, :], w_out[rs_, :])])))
            bg_ptr = [0]

            def run_bg(frac):
                tgt = min(len(bg_casts), int(round(frac * len(bg_casts))))
                while bg_ptr[0] < tgt:
                    bg_casts[bg_ptr[0]]()
                    bg_ptr[0] += 1
            ones_tok = [pool.tok(pool.h.memset(va_sb[:, :, :, 64:65], 1.0)),
                        pool.tok(pool.h.memset(vb_sb[:, :, :, 64:65], 1.0))]
        else:
            ds_w3 = [DSem(nc, f"p{which}_dsw3{i}", pes) for i in range(2)]
            tk_wo = ds_w3[0].issue(sp, [(wout[:], woutb.rearrange("(j p) d -> p j d", p=128))],
                                     waits=[P1.tk_wout])

        loads = []
        for c in range(NCH):
            for g in range(11):
                loads.append(([(0, wgb[which][g]),
                               (1, wub[which][g])],
                              [tk_wg[which][g], tk_wu[which][g]] if which == 0 else [tk_wg[which], tk_wu[which]]))
            if which == 0:
                for (c0,) in ((1536,), (2048,), (0,), (512,), (1024,)):
                    prs = [(0, winb[c0 // 256])]
                    if c0 != 2048:
                        prs.append((1, winb[c0 // 256 + 1]))
                    loads.append((prs, [tk_win]))
        slot_loaded = {}
        slot_free = {}
        issued = [0]

        def issue_load(k):
            if k >= len(loads):
                return
            prs, waits = loads[k]
            s = k % 3
            pairs = [(ws[s][:, half, :, :].rearrange("p cc f -> p (cc f)"), src) for (half, src) in prs]
            slot_loaded[k] = ds_ws[s].issue(sp, pairs, waits=list(waits) + [slot_free.get(k - 3)])

        load_ptr = [0]

        def next_slot():
            k = load_ptr[0]
            load_ptr[0] += 1
            return k, ws[k % 3], slot_loaded[k]

        def release_slot(k, tok):
            slot_free[k] = tok
            issue_load(k + 3)

        x_src = x if which == 0 else x1s
        tk_x = {}
        xt_free = {}
        aux = {}

        def issue_x(c):
            if c >= NCH:
                return
            b = c % 2
            qx = sp if c < 2 else pool
            dsx_, dsa_ = (ds_x[b], ds_aux[b]) if c < 2 else (ds_xp[b], ds_auxp[b])
            tk_x[c] = dsx_.issue(qx, [(xt[b][:], x_src[c * 512:(c + 1) * 512, :].rearrange("(t p) d -> p t d", p=128))],
                                    waits=[xt_free.get(c - 2)])
            if which == 0:
                aux[c] = dsa_.issue(qx, [
                    (c2t[b][:], c2_d[c * 512:(c + 1) * 512, :].rearrange("(t p) d -> p t d", p=128)),
                    (s2t[b][:], s2_d[c * 512:(c + 1) * 512, :].rearrange("(t p) d -> p t d", p=128))],
                    waits=[xt_free.get(c - 2)])
            else:
                aux[c] = dsa_.issue(qx, [(ot[b][:], oT_d[:, :, c * 512:(c + 1) * 512].rearrange("j p f -> p j f"))],
                                       waits=[xt_free.get(c - 2), P2_done])

        st.hn_free = [None, None]
        st.tp_free = [None, None]
        st.hT_free = [None, None]
        st.pg_free = [None, None]
        st.pu_free = [None, None]
        st.sg_free = [None, None]
        st.py_free = [None, None]
        st.actT_free = None
        st.py_i = 0

        class NormT:
            def __init__(self, src_tiles, src_ready, gt, hb, bmap=(0, 1, 0, 1)):
                self.src, self.rdy, self.gt, self.hb = src_tiles, src_ready, gt, hb
                self.bmap = bmap
                self.cols = stat_cols(12)
                self.th = [None] * 4
                self.ready = None

            def pre(self, t):
                c_ = self.cols
                ssq, lnv, rstd = c_[:, t:t + 1], c_[:, 4 + t:5 + t], c_[:, 8 + t:9 + t]
                b = self.bmap[t]
                act.wait(self.rdy[t])
                t0_ = act.tok(act.h.activation(out=junk[:], in_=self.src[t], func=AF.Square, accum_out=ssq))
                act.wait(t0_, tk_eps)
                tl = act.tok(act.h.activation(out=lnv, in_=ssq, func=AF.Ln, bias=epsc[:, 0:1], scale=1.0 / D))
                act.wait(tl)
                tr = act.tok(act.h.activation(out=rstd, in_=lnv, func=AF.Exp, scale=-0.5))
                dve.wait(tr, self.rdy[t], st.hn_free[b], tk_g)
                self.th[t] = dve.tok(dve.h.scalar_tensor_tensor(out=hn[b][:], in0=self.src[t], scalar=rstd,
                                                                in1=self.gt[:], op0=ALU.mult, op1=ALU.mult))

            def post(self, t):
                b = self.bmap[t]
                pe.wait(self.th[t], st.tp_free[b], tk_const)
                for j in range(8):
                    ins = pe.h.transpose(out=tp[b][:, j * 128:(j + 1) * 128], in_=hn[b][:, j * 128:(j + 1) * 128],
                                         identity=ident[:])
                tt = pe.tok(ins)
                st.hn_free[b] = tt
                act.wait(tt, st.hT_free[self.hb])
                te = act.tok(act.h.activation(out=hT[self.hb][:, :, t * 128:(t + 1) * 128],
                                              in_=tp[b][:, :].rearrange("p (j k) -> p j k", k=128), func=AF.Copy))
                st.tp_free[b] = te
                self.ready = te

            def run_all(self):
                self.pre(0)
                self.pre(1)
                self.post(0)
                self.pre(2)
                self.post(1)
                self.pre(3)
                self.post(2)
                self.post(3)
                return self.ready

        def norm_T(src_tiles, src_ready, gt, hb):
            return NormT(src_tiles, src_ready, gt, hb).run_all()

        def gate_up(hb, hT_ready):
            act_ready = None
            for fc in range(NFC):
                fl = fc % 2
                if fl == 0:
                    k, slot, tk_l = next_slot()
                b = fc % 2
                pe.wait(tk_l, hT_ready, st.pg_free[b])
                for cc in range(8):
                    ins = pe.h.matmul(pg[b][:], lhsT=slot[:, 0, cc, fl * 128:(fl + 1) * 128], rhs=hT[hb][:, cc, :],
                                      start=(cc == 0), stop=(cc == 7))
                tg = pe.tok(ins)
                pe.wait(st.pu_free[b])
                for cc in range(8):
                    ins = pe.h.matmul(pu[b][:], lhsT=slot[:, 1, cc, fl * 128:(fl + 1) * 128], rhs=hT[hb][:, cc, :],
                                      start=(cc == 0), stop=(cc == 7))
                tu = pe.tok(ins)
                if fl == 1:
                    release_slot(k, tu)
                act.wait(tg, st.sg_free[b])
                ts = act.tok(act.h.activation(out=sg[b][:], in_=pg[b][:], func=AF.Silu))
                st.pg_free[b] = ts
                dve.wait(ts, tu, st.actT_free)
                tm = dve.tok(dve.h.tensor_tensor(out=actT[:, fc, :], in0=sg[b][:], in1=pu[b][:], op=ALU.mult))
                st.pu_free[b] = tm
                st.sg_free[b] = tm
                act_ready = tm
                st.hT_free[hb] = tu
            return act_ready

        def down(xb, act_ready, cb=None):
            ready = []
            for t in range(4):
                if cb is not None and t > 0:
                    cb(t - 1, ready)
                for n in range(2):
                    b = st.py_i % 2
                    st.py_i += 1
                    pe.wait(act_ready, st.py_free[b], tk_wd)
                    for fc in range(NFC):
                        ins = pe.h.matmul(py[b][:], lhsT=actT[:, fc, t * 128:(t + 1) * 128],
                                          rhs=wd[:, fc, n * 512:(n + 1) * 512], start=(fc == 0), stop=(fc == NFC - 1))
                    ty = pe.tok(ins)
                    dve.wait(ty)
                    xs = xt[xb][:, t, n * 512:(n + 1) * 512]
                    tr_ = dve.tok(dve.h.scalar_tensor_tensor(out=xs, in0=py[b][:], scalar=0.5, in1=xs,
                                                            op0=ALU.mult, op1=ALU.add))
                    st.py_free[b] = tr_
                st.actT_free = ty
                ready.append(tr_)
            if cb is not None:
                cb(3, ready)
            return ready

        class StageA:
            def __init__(self, c, bmap=(0, 1, 0, 1)):
                self.c = c
                self.bmap = bmap
                self.xb = c % 2
                self.tiles = [xt[self.xb][:, t, :] for t in range(4)]
                self.rdy = [tk_x[c]] * 4
                self.nt = None

            def wout_tile(self, t):
                c, xb = self.c, self.xb
                for n in range(2):
                    b = st.py_i % 2
                    st.py_i += 1
                    pe.wait(aux[c], tk_wo, st.py_free[b])
                    for j in range(8):
                        ins = pe.h.matmul(py[b][:], lhsT=ot[xb][:, j, t * 128:(t + 1) * 128],
                                          rhs=wout[:, j, n * 512:(n + 1) * 512], start=(j == 0), stop=(j == 7))
                    ty = pe.tok(ins)
                    dve.wait(ty, tk_x[c])
                    xs = xt[xb][:, t, n * 512:(n + 1) * 512]
                    tr_ = dve.tok(dve.h.tensor_tensor(out=xs, in0=py[b][:], in1=xs, op=ALU.add))
                    st.py_free[b] = tr_
                self.rdy[t] = tr_
                st.ot_done = ty

            def step(self, k):
                if k == 0:
                    if which == 1:
                        self.rdy = [None] * 4
                        self.wout_tile(0)
                        self.wout_tile(1)
                    self.nt = NormT(self.tiles, self.rdy, gA, 0, self.bmap)
                    self.nt.pre(0)
                    self.nt.pre(1)
                elif k == 1:
                    if which == 1:
                        self.wout_tile(2)
                        self.wout_tile(3)
                    self.nt.post(0)
                    self.nt.pre(2)
                elif k == 2:
                    self.nt.post(1)
                    self.nt.pre(3)
                else:
                    self.nt.post(2)
                    self.nt.post(3)

            def run_all(self):
                for k in range(4):
                    self.step(k)
                return self.nt.ready

        def stage_A(c):
            return StageA(c).run_all()

        def evac_act(dst, src, waits):
            act.wait(*waits)
            return act.tok(act.h.activation(out=dst, in_=src, func=AF.Copy))

        def stage_D1(c, x1_ready):
            xb = c % 2
            tiles = [xt[xb][:, t, :] for t in range(4)]
            ck(f"p0_A{c+1}")
            if getattr(st, "n2", None) is not None:
                h2_ready = st.n2.ready
            else:
                h2_ready = norm_T(tiles, x1_ready, gB, 1)
            ck(f"p0_W{c}a")
            tk_x1st = ds_st[0].issue(pool, [(x1s[c * 512:(c + 1) * 512, :].rearrange("(t p) d -> p t d", p=128), xt[xb][:])],
                                     waits=[x1_ready[3]])
            stage_done = [tk_x1st]
            kD, slotD, tkD = next_slot()
            kE, slotE, tkE = next_slot()
            qk_ready = [[None, None, None] for _ in range(4)]
            ty_last = [None]
            tv_last = [None]

            def qk_mm(t):
                qb_ = t % 2
                for half in range(2):
                    b = st.py_i % 2
                    st.py_i += 1
                    pe.wait(tkD, h2_ready, st.py_free[b])
                    for cc in range(8):
                        ins = pe.h.matmul(py[b][:, 0:256], lhsT=hT[1][:, cc, t * 128:(t + 1) * 128],
                                          rhs=slotD[:, half, cc, :], start=(cc == 0), stop=(cc == 7))
                    ty = pe.tok(ins)
                    te = evac_act(qkf[qb_][:, half * 4:(half + 1) * 4, :],
                                  py[b][:, 0:256].rearrange("p (h d) -> p h d", d=64), [ty, st.qkf_free[qb_]])
                    st.py_free[b] = te
                    qk_ready[t][half] = te
                b = st.py_i % 2
                st.py_i += 1
                pe.wait(tkE, st.py_free[b])
                for cc in range(8):
                    ins = pe.h.matmul(py[b][:, 0:256], lhsT=hT[1][:, cc, t * 128:(t + 1) * 128],
                                      rhs=slotE[:, 0, cc, :], start=(cc == 0), stop=(cc == 7))
                ty = pe.tok(ins)
                te = evac_act(qkf[qb_][:, 8:10, :], py[b][:, 0:128].rearrange("p (h d) -> p h d", d=64), [ty])
                qk_ready[t][2] = te
                tv = evac_act(vb_sb[:, t, :, 0:64], py[b][:, 128:256].rearrange("p (h d) -> p h d", d=64),
                              [st.vb_free, ones_tok])
                st.py_free[b] = tv
                ty_last[0] = ty
                tv_last[0] = tv

            def qk_chain(t):
                qb_ = t % 2
                f = qkf[qb_]
                dve.wait(qk_ready[t])
                a1 = dve.tok(dve.h.tensor_tensor(out=qsq[:], in0=f[:], in1=f[:], op=ALU.mult))
                hs = stat_cols(10)
                hl = stat_cols(10)
                hr = stat_cols(10)
                dve.wait(a1)
                a2 = dve.tok(dve.h.tensor_reduce(out=hs, in_=qsq[:], axis=AX.X, op=ALU.add))
                act.wait(a2, tk_eps)
                a3 = act.tok(act.h.activation(out=hl, in_=hs, func=AF.Ln, bias=epsc[:, 0:1], scale=1.0 / 64))
                act.wait(a3)
                a4 = act.tok(act.h.activation(out=hr, in_=hl, func=AF.Exp, scale=-0.5))
                dve.wait(a4)
                a5 = dve.tok(dve.h.tensor_tensor(out=qrt[:], in0=f[:], in1=hr.unsqueeze(2).to_broadcast([128, 10, 64]),
                                                 op=ALU.mult))
                dve.wait(a5, tk_consts_ready)
                a6 = dve.tok(dve.h.tensor_tensor(out=qrt[:, 0:8, :], in0=qrt[:, 0:8, :],
                                                 in1=gq_t[:].unsqueeze(1).to_broadcast([128, 8, 64]), op=ALU.mult))
                a7 = dve.tok(dve.h.tensor_tensor(out=qrt[:, 8:10, :], in0=qrt[:, 8:10, :],
                                                 in1=gk_t[:].unsqueeze(1).to_broadcast([128, 2, 64]), op=ALU.mult))
                st.qkf_free[qb_] = a5
                dve.wait(a6, a7, aux[c])
                qv = qrt[:].rearrange("p h (i two) -> p h i two", two=2)
                tv_ = qtm[:].rearrange("p h (i two) -> p h i two", two=2)
                sv = s2t[xb][:, t, :].rearrange("p (i two) -> p i two", two=2)
                a8 = dve.tok(dve.h.tensor_tensor(out=tv_[:, :, :, 0], in0=qv[:, :, :, 1],
                                                 in1=sv[:, :, 0].unsqueeze(1).to_broadcast([128, 10, 32]), op=ALU.mult))
                a9 = dve.tok(dve.h.tensor_tensor(out=tv_[:, :, :, 1], in0=qv[:, :, :, 0],
                                                 in1=sv[:, :, 1].unsqueeze(1).to_broadcast([128, 10, 32]), op=ALU.mult))
                dve.wait(a8, a9)
                a10 = dve.tok(dve.h.tensor_tensor(out=qrt[:], in0=qrt[:],
                                                  in1=c2t[xb][:, t, :].unsqueeze(1).to_broadcast([128, 10, 64]), op=ALU.mult))
                dve.wait(a8, a9, a10, st.qkn_free[t])
                a11 = dve.tok(dve.h.tensor_tensor(
                    out=qkn[t][:, 0:512].rearrange("p (j hh d) -> p hh j d", hh=2, d=64),
                    in0=qrt[:, 0:8, :].rearrange("p (hh j) d -> p hh j d", hh=2),
                    in1=qtm[:, 0:8, :].rearrange("p (hh j) d -> p hh j d", hh=2), op=ALU.add))
                a12 = dve.tok(dve.h.tensor_tensor(out=qkn[t][:, 512:640].rearrange("p (h d) -> p h d", d=64),
                                                  in0=qrt[:, 8:10, :], in1=qtm[:, 8:10, :], op=ALU.add))
                st.qkn_ready[t] = [a11, a12]

            def feat_slot(dst_sb, nm):
                kk, slot, tkl = next_slot()
                for half in range(2):
                    for jj in range(2):
                        j = half * 2 + jj
                        b = st.py_i % 2
                        st.py_i += 1
                        pe.wait(tkl, st.py_free[b])
                        for cc in range(8):
                            ins = pe.h.matmul(py[b][:], lhsT=slot[:, half, cc, jj * 128:(jj + 1) * 128],
                                              rhs=hT[1][:, cc, :], start=(cc == 0), stop=(cc == 7))
                        ty = pe.tok(ins)
                        te = evac_act(dst_sb[:, j, :], py[b][:], [ty, st.stg_free.get(nm)])
                        st.py_free[b] = te
                release_slot(kk, ty)
                st.stg_ready[nm] = te

            qk_mm(0)
            qk_mm(1)
            qk_chain(0)
            qk_mm(2)
            qk_chain(1)
            qk_mm(3)
            release_slot(kD, ty_last[0])
            release_slot(kE, ty_last[0])
            tk_vb_ready = tv_last[0]
            feat_slot(qaT_sb, "qa")
            qk_chain(2)
            feat_slot(kaT_sb, "ka")
            qk_chain(3)
            kC, slotC, tkC = next_slot()
            for t in range(4):
                for half in range(2):
                    b = st.py_i % 2
                    st.py_i += 1
                    pe.wait(tkC, st.py_free[b])
                    for cc in range(8):
                        ins = pe.h.matmul(py[b][:, 0:256], lhsT=hT[1][:, cc, t * 128:(t + 1) * 128],
                                          rhs=slotC[:, half, cc, :], start=(cc == 0), stop=(cc == 7))
                    ty = pe.tok(ins)
                    te = evac_act(va_sb[:, t, half * 4:(half + 1) * 4, 0:64],
                                  py[b][:, 0:256].rearrange("p (h d) -> p h d", d=64), [ty, st.va_free, ones_tok])
                    st.py_free[b] = te
            release_slot(kC, ty)
            st.hT_free[1] = ty
            tk_va_ready = te
            ck(f"p0_W{c}d")
            for t in range(4):
                qb_ = t % 2
                b = t % 2
                pe.wait(st.qkn_ready[t], st.tp_free[b])
                for j in range(5):
                    ins = pe.h.transpose(out=tp[b][:, j * 128:(j + 1) * 128], in_=qkn[t][:, j * 128:(j + 1) * 128],
                                         identity=ident[:])
                tt = pe.tok(ins)
                st.qkn_free[t] = tt
                dve.wait(tt, st.stg_free.get("qb"))
                e1 = dve.tok(dve.h.tensor_copy(out=qbT_sb[:, :, t * 128:(t + 1) * 128],
                                               in_=tp[b][:, 0:512].rearrange("p (j k) -> p j k", k=128)))
                e2 = dve.tok(dve.h.tensor_copy(out=kbT_sb[:, t * 128:(t + 1) * 128], in_=tp[b][:, 512:640]))
                st.tp_free[b] = [e1, e2]
            ck(f"p0_W{c}e")
            sl = slice(c * 512, (c + 1) * 512)
            s1 = ds_st[1].issue(pool, [
                (qaT_d[:, :, sl].rearrange("j p f -> p j f"), qaT_sb[:]),
                (kaT_d[:, :, sl].rearrange("j p f -> p j f"), kaT_sb[:])],
                waits=[st.stg_ready["qa"], st.stg_ready["ka"]])
            s2 = ds_st[2].issue(pool, [
                (va_d[sl, :].rearrange("(t p) e -> p t e", p=128), va_sb[:].rearrange("p t h e -> p t (h e)")),
                (vb_d[sl, :].rearrange("(t p) e -> p t e", p=128), vb_sb[:].rearrange("p t h e -> p t (h e)"))],
                waits=[tk_va_ready, tk_vb_ready])
            s3 = ds_st[3].issue(pool, [
                (qbT_d[:, :, sl].rearrange("j p f -> p j f"), qbT_sb[:]),
                (kbT_d[:, sl], kbT_sb[:])],
                waits=[e1, e2])
            st.stg_free["qa"] = s1
            st.stg_free["ka"] = s1
            st.va_free = s2
            st.vb_free = s2
            st.stg_free["qb"] = s3
            xt_free[c] = [tk_x1st, h2_ready, s3]
            st.all_stores = [tk_x1st, s1, s2, s3]

        def stage_F3(c, x3_ready):
            xb = c % 2
            ssq = stat_cols(4)
            lnv = stat_cols(4)
            rstd = stat_cols(4)
            tks = []
            for t in range(4):
                act.wait(x3_ready[t])
                if tks:
                    act.wait(tks[-1])
                tks.append(act.tok(act.h.activation(out=junk[:], in_=xt[xb][:, t, :], func=AF.Square,
                                                    accum_out=ssq[:, t:t + 1])))
            act.wait(tks[-1], tk_eps)
            tl = act.tok(act.h.activation(out=lnv, in_=ssq, func=AF.Ln, bias=epsc[:, 0:1], scale=1.0 / D))
            act.wait(tl)
            tr = act.tok(act.h.activation(out=rstd, in_=lnv, func=AF.Exp, scale=-0.5))
            for t in range(4):
                dve.wait(tr, x3_ready[t], tk_g)
                tn = dve.tok(dve.h.scalar_tensor_tensor(out=xt[xb][:, t, :], in0=xt[xb][:, t, :], scalar=rstd[:, t:t + 1],
                                                        in1=gB[:], op0=ALU.mult, op1=ALU.mult))
            tso = ds_st[0].issue(pool, [(out[c * 512:(c + 1) * 512, :].rearrange("(t p) d -> p t d", p=128), xt[xb][:])],
                                 waits=[tn])
            xt_free[c] = [tso]
            st.all_stores = [tso]

        if which == 0:
            st.qkf_free = [None, None]
            st.qkn_free = [None] * 4
            st.qkn_ready = [None] * 4
            st.stg_free = {}
            st.stg_ready = {}
            st.va_free = None
            st.vb_free = None

        ck(f"p{which}_prologue")
        issue_x(0)
        for k in range(3):
            issue_load(k)
        if which == 1:
            tk_wd = ds_w3[1].issue(sp, [(wd[:], wd2b.rearrange("(fc p) d -> p fc d", p=128))],
                                     waits=[P1.tk_wd2])
        issue_x(1)
        ck(f"p{which}_loads")
        hT_ready = stage_A(0)
        ck(f"p{which}_A0")
        for c in range(NCH):
            act_ready = gate_up(0, hT_ready)
            ck(f"p{which}_G{c}")
            if which == 0 and c >= 1:
                run_bg((c + 0.4) / NCH)
            nxt = StageA(c + 1, (0, 1, 0, 1) if (which == 1 or NCH == 1) else (0, 1, 1, 1)) if c + 1 < NCH else None
            n2 = [None]

            def d_cb(k, rdy, c=c, nxt=nxt, n2=n2):
                if which == 1 or NCH == 1:
                    if nxt is not None:
                        nxt.step(k)
                    return
                if k == 0:
                    if nxt is not None:
                        nxt.step(0)
                    n2[0] = NormT([xt[c % 2][:, t, :] for t in range(4)], rdy, gB, 1, (0, 0, 0, 1))
                elif k == 1:
                    if nxt is not None:
                        nxt.nt.post(0)
                        nxt.nt.post(1)
                    n2[0].pre(0)
                    if nxt is not None:
                        nxt.nt.pre(2)
                elif k == 2:
                    n2[0].post(0)
                    if nxt is not None:
                        nxt.nt.post(2)
                    n2[0].pre(1)
                    if nxt is not None:
                        nxt.nt.pre(3)
                else:
                    n2[0].post(1)
                    if nxt is not None:
                        nxt.nt.post(3)
                    n2[0].pre(2)
                    n2[0].pre(3)
                    pe.wait(st.pg_free[0])
                    for _d in range(16):
                        kw_ = pe.h.matmul(pg[0][:], lhsT=actT[:, 0, 0:128], rhs=actT[:, 1, :], start=True, stop=True)
                    st.actT_free = [st.actT_free, pe.tok(kw_)]
                    n2[0].post(2)
                    n2[0].post(3)

            ready = down(c % 2, act_ready, cb=d_cb)
            st.n2 = n2[0]
            ck(f"p{which}_D{c}")
            if which == 0 and c >= 1:
                run_bg((c + 0.7) / NCH)
            if which == 0:
                if nxt is not None:
                    hT_ready = nxt.nt.ready
                stage_D1(c, ready)
                ck(f"p{which}_W{c}")
                run_bg((c + 1.0) / NCH)
            else:
                if nxt is not None:
                    hT_ready = nxt.nt.ready
                stage_F3(c, ready)
            issue_x(c + 2)
        fin = list(st.all_stores)
        for e in (pe, act, dve):
            pass
        st.final = fin
        st.exit = pes
        return st

    def attention_phase():
        pes = ExitStack()

        def psb(name, shape, dt):
            return pes.enter_context(nc.sbuf_tensor("a_" + name, list(shape), dt))

        def pps(name, shape, dt):
            return pes.enter_context(nc.psum_tensor("a_" + name, list(shape), dt))

        kaT = psb("kaT", [128, 4, S], BF16)
        kbT = psb("kbT", [128, S], BF16)
        var = psb("var", [128, NT, 520], BF16)
        vbr = psb("vbr", [128, NT, 130], BF16)
        Z = psb("Z", [128, 8, ZW], F32)
        qa = [psb("qa0", [128, 4, 512], BF16), psb("qa1", [128, 4, 512], BF16)]
        qb = [psb("qb0", [128, 4, 512], BF16), psb("qb1", [128, 4, 512], BF16)]
        NPT = 6
        pT = [psb(f"pT{i}", [128, 1024], BF16) for i in range(NPT)]
        pTf = [psb(f"pTf{i}", [128, 1024], F32) for i in range(2)]
        accs = psb("accs", [65, 1024], F32)
        Rr = psb("Rr", [64, 1024], F32)
        t12 = psb("t12", [64, 1024], F32)
        o_ = psb("o_", [64, 512], F32)
        sq_ = psb("sq_", [64, 512], BF16)
        at4l = [psb("at4l0", [128, 4], F32), psb("at4l1", [128, 4], F32)]
        at4 = [psb("at40", [128, 4], F32), psb("at41", [128, 4], F32)]
        lt_free2 = [None, None]
        alp = psb("alp", [64, 512], F32)
        fin = [psb("fin0", [64, 1024], BF16), psb("fin1", [64, 1024], BF16)]
        ones64 = psb("ones64", [64, 64], BF16)
        jmat = psb("jmat", [128, 128], F32)
        zt = psb("zt", [128, 512], BF16)
        lt = [psb("lt0", [128, 8], F32), psb("lt1", [128, 8], F32)]
        ltr = [psb("ltr0", [128, 8], F32), psb("ltr1", [128, 8], F32)]
        lt_free = [None, None]

        stp = [pps("st0", [128, 1024], F32), pps("st1", [128, 1024], F32)]
        accP = [pps("accA", [128, 1024], F32), pps("accB", [128, 1024], F32)]

        ds_kv = DSem(nc, "a_dskv", pes)
        ds_q = [DSem(nc, f"a_dsq{i}", pes) for i in range(2)]
        ds_z = DSem(nc, "a_dsz", pes)
        ds_z2 = DSem(nc, "a_dsz2", pes)
        ds_o = [DSem(nc, f"a_dso{i}", pes) for i in range(2)]
        ds_e = [DSem(nc, f"a_dse{i}", pes) for i in range(4)]
        ds_e2 = [DSem(nc, f"a_dsf{i}", pes) for i in range(2)]

        p1_done = P1.final
        for e_ in (pe, act, dve, pool, sp):
            e_.wait(p1_done)
        NKV = 4 if NT % 4 == 0 else 1
        KTP = NT // NKV
        ds_kvp = [DSem(nc, f"a_dskv{i}", pes) for i in range(NKV)]
        tk_kvp = []

        m3 = dve.tok(dve.h.memset(ones64[:], 1.0))
        m4 = dve.tok(dve.h.memset(zt[:], 0.0))
        tk_sel = [m3, m4]
        st_free0 = None
        def load_kv_piece(i_):
            ts_ = slice(i_ * KTP * 128, (i_ + 1) * KTP * 128)
            kk_ = slice(i_ * KTP, (i_ + 1) * KTP)
            tk_kvp.append(ds_kvp[i_].issue(sp, [
                (kaT[:, :, ts_], kaT_d[:, :, ts_].rearrange("j p s -> p j s")),
                (kbT[:, ts_], kbT_d[:, ts_]),
                (var[:, kk_, :], va_d[ts_, :].rearrange("(kt p) e -> p kt e", p=128)),
                (vbr[:, kk_, :], vb_d[ts_, :].rearrange("(kt p) e -> p kt e", p=128))], waits=p1_done))

        tk_q = {}
        q_free = {}

        def issue_q(qc):
            if qc >= NCH:
                return
            b = qc % 2
            sl = slice(qc * 512, (qc + 1) * 512)
            tk_q[qc] = ds_q[b].issue(sp, [
                (qa[b][:], qaT_d[:, :, sl].rearrange("j p f -> p j f")),
                (qb[b][:], qbT_d[:, :, sl].rearrange("j p f -> p j f"))], waits=list(p1_done) + [q_free.get(qc - 2)])

        issue_q(0)
        load_kv_piece(0)
        tk_zh = ds_z2.issue(sp, [(Z[:], bass.AP(vbias_d.tensor, vbias_d.offset, [[1, 128], [VL, 8], [1, ZW]])),
                                 (jmat[:], jmat_d)], waits=[tk_vst])
        for i_ in range(1, NKV):
            load_kv_piece(i_)
        issue_q(1)
        zi = 0
        zfree = [None, None]
        for hh in range(8):
            for (a_, b_) in ((0, 512), (512, 1024), (1024, ZW)):
                zb = stp[zi % 2]
                pe.wait(tk_zh, zfree[zi % 2])
                tzz = pe.tok(pe.h.matmul(zb[:, 0:b_ - a_], lhsT=jmat[:], rhs=Z[:, hh, a_:b_], start=True, stop=True))
                dve.wait(tzz)
                tk_Z = dve.tok(dve.h.tensor_copy(out=Z[:, hh, a_:b_], in_=zb[:, 0:b_ - a_]))
                zfree[zi % 2] = tk_Z
                zi += 1
        st_free0 = [zfree[0], zfree[1]]
        tk_EZ = tk_Z

        units = []
        grp = 0
        for qc in range(NCH):
            near = [kt for kt in range(NT) if -256 < kt * 128 - qc * 512 < 640]
            far = [kt for kt in range(NT) if kt not in near]
            order = []
            step = max(1, len(far) // max(1, len(near)))
            ni = 0
            for fi, kt in enumerate(far):
                order.append(kt)
                if (fi + 1) % step == 0 and ni < len(near):
                    order.append(near[ni])
                    ni += 1
            order.extend(near[ni:])
            assert sorted(order) == list(range(NT))
            for h in range(8):
                for oi, kt in enumerate(order):
                    units.append(("A", qc, h, kt, grp, oi == 0, oi == NT - 1))
                grp += 1
            for j in range(4):
                for kt in range(NT):
                    units.append(("B", qc, j, kt, grp, kt == 0, kt == NT - 1))
                grp += 1

        st_free = [st_free0, st_free0]
        pT_free = [None] * NPT
        exp_done = {}
        near_i = [0]
        pTf_free = [None, None]
        pair_free = [[], []]
        tmp_free = {}
        pending = []
        o_stores = []
        fin_free = [None, None]
        fin_i = [0]
        last_pv = [None]

        def emit_qk(i):
            kind, qc, hj, kt = units[i][:4]
            s = i % 2
            qbuf = qc % 2
            pe.wait(st_free[s], tk_kvp[kt // KTP], tk_q[qc], tk_sel)
            ks = slice(kt * 128, (kt + 1) * 128)
            for _d in range(N_DUMMY):
                pe.h.matmul(stp[s][:, 0:DUMMY_W], lhsT=zt[:, 0:128], rhs=zt[:, 0:DUMMY_W], start=True, stop=True)
            if kind == "A":
                jc, rb = hj // 2, (hj % 2) * 64
                for m in range(2):
                    r0 = rb + 32 * m
                    ins = pe.h.matmul(stp[s][:, m * 512:(m + 1) * 512], lhsT=kaT[r0:r0 + 32, jc, ks],
                                      rhs=qa[qbuf][r0:r0 + 32, jc, :], start=True, stop=True, tile_position=(r0, 0))
            else:
                for m in range(2):
                    r0 = 64 * m
                    ins = pe.h.matmul(stp[s][:, m * 512:(m + 1) * 512], lhsT=kbT[r0:r0 + 64, ks],
                                      rhs=qb[qbuf][r0:r0 + 64, hj, :], start=True, stop=True, tile_position=(r0, 0))
            return pe.tok(ins)

        def emit_exp(i, tk_qk):
            kind, qc, hj, kt = units[i][:4]
            s = i % 2
            ps = i % NPT
            if kind == "A":
                delta = kt * 128 - qc * 512
                if -256 < delta < 640:
                    x0 = 512 - delta
                    nb = near_i[0] % 2
                    near_i[0] += 1
                    act.wait(tk_qk, pTf_free[nb])
                    te = act.tok(act.h.activation(out=pTf[nb][:], in_=stp[s][:], func=AF.Exp, scale=SC_A))
                    st_free[s] = te
                    dve.wait(te, tk_EZ, pT_free[ps], tk_consts_ready)
                    td = dve.tok(dve.h.scalar_tensor_tensor(
                        out=pT[ps][:, :].rearrange("p (m f) -> p m f", m=2),
                        in0=pTf[nb][:, :].rearrange("p (m f) -> p m f", m=2), scalar=cneg[:, hj:hj + 1],
                        in1=Z[:, hj, x0:x0 + 512].unsqueeze(1).to_broadcast([128, 2, 512]),
                        op0=ALU.mult, op1=ALU.mult))
                    pTf_free[nb] = td
                    exp_done[i] = td
                    return
                else:
                    act.wait(tk_qk, pT_free[ps], tk_consts_ready)
                    if delta < 0:
                        ins = act.h.activation(out=pT[ps][:], in_=stp[s][:], func=AF.Exp, scale=SC_A)
                    else:
                        ins = act.h.activation(out=pT[ps][:], in_=stp[s][:], func=AF.Exp, bias=bdiff[:, hj:hj + 1],
                                               scale=SC_A)
            else:
                act.wait(tk_qk, pT_free[ps])
                ins = act.h.activation(out=pT[ps][:], in_=stp[s][:], func=AF.Exp)
            te = act.tok(ins)
            st_free[s] = te
            exp_done[i] = te

        def emit_pv(i):
            kind, qc, hj, kt, g, first, last = units[i]
            ps = i % NPT
            acc = accP[g % 2]
            pe.wait(exp_done.pop(i))
            if first:
                pe.wait(pair_free[g % 2])
            for m in range(2):
                if kind == "A":
                    lh = var[:, kt, hj * 65:(hj + 1) * 65]
                else:
                    lh = vbr[:, kt, m * 65:(m + 1) * 65]
                ins = pe.h.matmul(acc[0:65, m * 512:(m + 1) * 512], lhsT=lh, rhs=pT[ps][:, m * 512:(m + 1) * 512],
                                  start=first, stop=last)
            tp_ = pe.tok(ins)
            pT_free[ps] = tp_
            last_pv[0] = tp_
            if last:
                schedule_epilogue(i, kind, qc, hj, tp_, g)

        def schedule_epilogue(i, kind, qc, hj, tk_pv, g):
            sv = {}
            acc = accP[g % 2]
            ep = [acc[:, 0:512], acc[:, 512:1024]]
            ep_free = [None, None]
            pair_free[g % 2] = []

            par = g % 2

            def e1():
                dve.wait(tk_pv, tmp_free.get("accs"))
                sv["c"] = dve.tok(dve.h.tensor_copy(out=accs[:], in_=acc[0:65, :]))
                pair_free[g % 2].append(sv["c"])
                d1 = ds_e[0].issue(sp, [(lrow_d[par:par + 1, :], accs[64:65, :])], waits=[sv["c"]])
                sv["d2"] = ds_e[1].issue(sp, [(lt[par][:], lrow_d[par, :].rearrange("(p f) -> p f", f=8))],
                                         waits=[d1, lt_free[par]])

            def e2():
                dve.wait(sv["d2"])
                v1 = dve.tok(dve.h.reciprocal(out=ltr[par][:], in_=lt[par][:]))
                lt_free[par] = v1
                d3 = ds_e[2].issue(sp, [(rrow_d[par, :].rearrange("(p f) -> p f", f=8), ltr[par][:])], waits=[v1])
                sv["d4"] = ds_e[3].issue(sp, [(Rr[:], bass.AP(rrow_d.tensor, rrow_d.offset + par * 1024,
                                                               [[0, 64], [1, 1024]]))],
                                         waits=[d3, tmp_free.get("Rr")])

            def e3():
                dve.wait(sv["d4"])
                if kind == "A":
                    dve.wait(tmp_free.get("t12"))
                    m_ = dve.tok(dve.h.tensor_tensor(out=t12[:], in0=accs[0:64, :], in1=Rr[:], op=ALU.mult))
                    tmp_free["accs"] = m_
                    tmp_free["Rr"] = m_
                    dve.wait(m_, tmp_free.get("o_"), tk_consts_ready)
                    oo = dve.tok(dve.h.scalar_tensor_tensor(out=o_[:], in0=t12[:, 512:1024], scalar=neglam[0:64, 0:1],
                                                            in1=t12[:, 0:512], op0=ALU.mult, op1=ALU.add))
                    tmp_free["t12"] = oo
                    dve.wait(oo, tmp_free.get("sq_"))
                    sv["sq"] = dve.tok(dve.h.tensor_tensor(out=sq_[:], in0=o_[:], in1=o_[:], op=ALU.mult))
                else:
                    fb = fin_i[0] % 2
                    fin_i[0] += 1
                    sv["fb"] = fb
                    dve.wait(fin_free[fb])
                    m_ = dve.tok(dve.h.tensor_tensor(out=fin[fb][:], in0=accs[0:64, :], in1=Rr[:], op=ALU.mult))
                    tmp_free["accs"] = m_
                    tmp_free["Rr"] = m_
                    sl = slice(qc * 512, (qc + 1) * 512)
                    c1, c2 = 4 + hj // 2, 6 + hj // 2
                    po = (hj % 2) * 64
                    tko = ds_o[fb].issue(pool, [(oT_d[c1, po:po + 64, sl], fin[fb][:, 0:512]),
                                                (oT_d[c2, po:po + 64, sl], fin[fb][:, 512:1024])], waits=[m_])
                    fin_free[fb] = tko
                    o_stores.append(tko)

            def e4():
                pe.wait(sv["sq"], sv["c"], tk_sel)
                sqv = sq_[:, :].rearrange("d (p f) -> d f p", f=4)
                for tb in range(4):
                    ins = pe.h.matmul(ep[0][:, tb:tb + 1], lhsT=sqv[:, tb, :], rhs=ones64[:, 0:1], start=True, stop=True)
                sv["ss"] = pe.tok(ins)

            def e5():
                act.wait(sv["ss"], lt_free2[par], tk_eps)
                l_ = act.tok(act.h.activation(out=at4l[par][:], in_=ep[0][:, 0:4], func=AF.Ln, bias=epsc[:, 0:1],
                                              scale=1.0 / 64))
                ep_free[0] = l_
                pair_free[g % 2].append(l_)
                tmp_free["sq_"] = sv["ss"]
                act.wait(l_)
                a_ = act.tok(act.h.activation(out=at4[par][:], in_=at4l[par][:], func=AF.Exp, scale=-0.5))
                d5 = ds_e2[0].issue(sp, [(arow_d[par, :].rearrange("(p f) -> p f", f=4), at4[par][:])], waits=[a_])
                lt_free2[par] = d5
                sv["al"] = ds_e2[1].issue(sp, [(alp[:], bass.AP(arow_d.tensor, arow_d.offset + par * 512,
                                                                [[0, 64], [1, 512]]))],
                                          waits=[d5, tmp_free.get("alp")])

            def e6():
                fb = fin_i[0] % 2
                fin_i[0] += 1
                dve.wait(sv["al"], fin_free[fb], tk_consts_ready)
                f_ = dve.tok(dve.h.scalar_tensor_tensor(out=fin[fb][:, 0:512], in0=o_[:], scalar=gsubc[:, 0:1],
                                                        in1=alp[:], op0=ALU.mult, op1=ALU.mult))
                tmp_free["o_"] = f_
                tmp_free["alp"] = f_
                sl = slice(qc * 512, (qc + 1) * 512)
                po = (hj % 2) * 64
                tko = ds_o[fb].issue(pool, [(oT_d[hj // 2, po:po + 64, sl], fin[fb][:, 0:512])], waits=[f_])
                fin_free[fb] = tko
                o_stores.append(tko)

            offs = (1, 7, 14, 17, 19, 25) if NT >= 24 else (1, 2, 4, 5, 6, 7)
            pending.append((i + offs[0], e1))
            pending.append((i + offs[1], e2))
            pending.append((i + offs[2], e3))
            if kind == "A":
                pending.append((i + offs[3], e4))
                pending.append((i + offs[4], e5))
                pending.append((i + offs[5], e6))

        n_units = len(units)
        PV_LAG = 3
        per_qc = 12 * NT
        for i in range(n_units + 32):
            if i < n_units:
                tk_qk = emit_qk(i)
                emit_exp(i, tk_qk)
            if PV_LAG <= i <= n_units + PV_LAG - 1:
                emit_pv(i - PV_LAG)
                if ((i - PV_LAG + 1) % per_qc) == 0:
                    qc_done = (i - PV_LAG + 1) // per_qc - 1
                    q_free[qc_done] = last_pv[0]
                    issue_q(qc_done + 2)
            still = []
            for (at, fn) in pending:
                if at <= i:
                    fn()
                else:
                    still.append((at, fn))
            pending[:] = still
        assert not pending
        res = K()
        res.done = [ds_o[0].last, ds_o[1].last, last_pv[0]]
        res.all_o = o_stores
        res.exit = pes
        return res

    del _ALL_DSEMS[:]
    _ALL_DSEMS.extend([ds_const] + ds_cast)
    try:
        P1 = token_phase(0)
    except _Stop as e_:
        print("STOP at", e_)
        for e in (pe, act, dve, pool):
            if e.n:
                sp.wait((e.name, e.sem, e.n))
        for d_ in _ALL_DSEMS:
            sp.wait(d_.last)
        return nc
    P1.exit.close()
    if 2 in phases:
        P2 = attention_phase()
        P2_done = P2.done
        P2.exit.close()
    else:
        P2_done = []
    if 3 in phases:
        P3 = token_phase(1)
        final_toks = P3.final
        P3.exit.close()
    else:
        final_toks = P1.final + list(P2_done)
    sp.wait(final_toks)
    es.close()
    return nc


def _t5_bucket(rel):
    nb = 16
    max_exact = 8
    ret = np.where(rel > 0, nb, 0)
    n = np.abs(rel)
    nf = np.maximum(n, 1).astype(np.float32)
    large = max_exact + (np.log(nf / np.float32(max_exact)) / np.float32(math.log(128 / max_exact))
                         * np.float32(nb - max_exact)).astype(np.int32)
    large = np.minimum(large, nb - 1)
    return ret + np.where(n < max_exact, n, large)


def _constants(S):
    ident = np.eye(128, dtype=np.float32).astype(ml_dtypes.bfloat16)
    u = np.arange(VL)
    bk = _t5_bucket(639 - u)
    ohm = np.zeros((32, VL), np.float32)
    ohm[bk, u] = 1.0
    rows = S // 64
    row = np.repeat(np.arange(rows), 64).astype(np.float32)
    col = np.tile(np.arange(64), rows).astype(np.float32)
    inv = (np.float32(10000.0) ** (-np.arange(0, 32, 2, dtype=np.float32) / np.float32(32))).astype(np.float32)
    ang = np.concatenate([row[:, None] * inv[None], col[:, None] * inv[None]], axis=-1).astype(np.float32)
    cos, sin = np.cos(ang).astype(np.float32), np.sin(ang).astype(np.float32)
    c2 = np.repeat(cos, 2, axis=1)
    s2 = np.stack([-sin, sin], axis=-1).reshape(S, 64)
    jm = np.ascontiguousarray(np.eye(128, dtype=np.float32)[::-1])
    return dict(ident=ident, ohm=ohm, jmat=jm, c2=np.ascontiguousarray(c2), s2=np.ascontiguousarray(s2))


_W1 = ("ffn1_w_gate", "ffn1_w_up", "ffn1_w_down", "ffn2_w_gate", "ffn2_w_up", "ffn2_w_down", "w_in", "w_out",
       "ffn1_norm", "ffn2_norm", "mix_norm", "lambda_q1", "lambda_k1", "lambda_q2", "lambda_k2",
       "diff_subln", "q_norm", "k_norm")


def _common_inputs(inputs, S):
    m = {}
    for k in _W1:
        a = np.asarray(inputs[k], dtype=np.float32)
        m[k] = np.ascontiguousarray(a[0])[None] if a[0].ndim == 1 else np.ascontiguousarray(a[0])
    m["final_norm"] = np.ascontiguousarray(np.asarray(inputs["final_norm"], np.float32))[None]
    m["rel_bias"] = np.ascontiguousarray(np.asarray(inputs["rel_bias"], np.float32))
    m.update(_constants(S))
    return m


def kernel(**inputs):
    x = np.asarray(inputs["x"], dtype=np.float32)
    B, S, _ = x.shape
    nc = build_program(S)
    common = _common_inputs(inputs, S)
    in_maps = []
    for b in range(B):
        m = dict(common)
        m["x"] = np.ascontiguousarray(x[b])
        in_maps.append(m)
    res = run_bass_kernel_spmd(nc, in_maps, core_ids=list(range(B)))
    return np.stack([np.asarray(r["out"], dtype=np.float32) for r in res.results], axis=0)
```

```python
import math
from contextlib import ExitStack

import numpy as np
import ml_dtypes

import concourse.bass as bass
import concourse.mybir as mybir
from concourse.bass_utils import run_bass_kernel_spmd

F32 = mybir.dt.float32
BF16 = mybir.dt.bfloat16
AF = mybir.ActivationFunctionType
ALU = mybir.AluOpType
AX = mybir.AxisListType

D = 1024
FF = 2816
NFC = FF // 128
DIN = 2304
EPS = 1e-6
SEQ = 4096
N_CORES = 8
LAMBDA_INIT = 0.8 - 0.6 * math.exp(-0.3 * 0)
SC_A = 32 ** -0.5
import os as _os0
N_DUMMY = int(_os0.environ.get('K_NDUMMY', '1'))
DUMMY_W = int(_os0.environ.get('K_DW', '320'))
FAST_RECIP = int(_os0.environ.get('K_FR', '0'))
ZW = 1152
VL = 1280


class Eng:
    def __init__(self, nc, h, name, es):
        self.h = h
        self.name = name
        self.sem = es.enter_context(nc.semaphore(name + "_sem"))
        self.n = 0
        self.seen = {}

    def wait(self, *toks):
        for t in toks:
            if t is None:
                continue
            if isinstance(t, (list,)):
                self.wait(*t)
                continue
            key, sem, val = t
            if self.seen.get(key, 0) >= val:
                continue
            self.h.wait_ge(sem, val)
            self.seen[key] = val

    def tok(self, ins):
        self.n += 1
        ins.then_inc(self.sem, 1)
        return (self.name, self.sem, self.n)


class DSem:
    def __init__(self, nc, name, es):
        self.name = name
        self.sem = es.enter_context(nc.semaphore(name))
        self.count = 0
        self.last = None
        _ALL_DSEMS.append(self)

    def issue(self, q, pairs, waits=()):
        q.wait(*waits)
        q.wait(self.last)
        if q.name == "pool":
            if len(_POOL_OUT) >= 2:
                q.wait(_POOL_OUT[-2])
        for (o, i) in pairs:
            q.h.dma_start(out=o, in_=i).then_inc(self.sem, 16)
            self.count += 16
        self.last = (self.name, self.sem, self.count)
        if q.name == "pool":
            _POOL_OUT.append(self.last)
        return self.last


class K:
    pass


class _Stop(Exception):
    pass


_ALL_DSEMS = []
_POOL_OUT = []
_STOP_AT = [None]


_CK_COUNT = {}


def ck(name):
    if _STOP_AT[0] is None:
        return
    tgt = _STOP_AT[0]
    n = 1
    if "#" in tgt:
        tgt, n = tgt.split("#")
        n = int(n)
    if tgt == name:
        _CK_COUNT[name] = _CK_COUNT.get(name, 0) + 1
        if _CK_COUNT[name] == n:
            raise _Stop(name)


def build_program(S=SEQ, debug=False, phases=(1, 2, 3)):
    NT = S // 128
    NCH = S // 512
    nc = bass.Bass("TRN2", target_bir_lowering=False)
    es = ExitStack()
    del _POOL_OUT[:]

    def din(name, shape, dt=F32):
        return nc.dram_tensor(name, list(shape), dt, kind="ExternalInput").ap()

    def dscr(name, shape, dt):
        kind = "ExternalOutput" if debug else "Internal"
        return nc.dram_tensor(name, list(shape), dt, kind=kind).ap()

    x = din("x", [S, D])
    w_g = [din("ffn1_w_gate", [D, FF]), din("ffn2_w_gate", [D, FF])]
    w_u = [din("ffn1_w_up", [D, FF]), din("ffn2_w_up", [D, FF])]
    w_d = [din("ffn1_w_down", [FF, D]), din("ffn2_w_down", [FF, D])]
    g_ffn = [din("ffn1_norm", [1, D]), din("ffn2_norm", [1, D])]
    g_mix = din("mix_norm", [1, D])
    g_fin = din("final_norm", [1, D])
    w_in = din("w_in", [D, DIN])
    w_out = din("w_out", [D, D])
    lam_in = [din(n, [1, 32]) for n in ("lambda_q1", "lambda_k1", "lambda_q2", "lambda_k2")]
    g_sub = din("diff_subln", [1, 64])
    g_q = din("q_norm", [1, 64])
    g_k = din("k_norm", [1, 64])
    rel_bias = din("rel_bias", [32, 8])
    ident_d = din("ident", [128, 128], BF16)
    ohm_d = din("ohm", [32, VL])
    jmat_d = din("jmat", [128, 128])
    c2_d = din("c2", [S, 64])
    s2_d = din("s2", [S, 64])
    out = nc.dram_tensor("out", [S, D], F32, kind="ExternalOutput").ap()

    wgb = [dscr("wg1b", [11, 128, 2048], BF16), dscr("wg2b", [11, 128, 2048], BF16)]
    wub = [dscr("wu1b", [11, 128, 2048], BF16), dscr("wu2b", [11, 128, 2048], BF16)]
    wd2b = dscr("wd2b", [FF, D], BF16)
    winb = dscr("winb", [9, 128, 2048], BF16)
    woutb = dscr("woutb", [D, D], BF16)
    x1s = dscr("x1s", [S, D], F32)
    qaT_d = dscr("qaT", [4, 128, S], BF16)
    kaT_d = dscr("kaT", [4, 128, S], BF16)
    va_d = dscr("va", [S, 520], BF16)
    qbT_d = dscr("qbT", [4, 128, S], BF16)
    kbT_d = dscr("kbT", [128, S], BF16)
    vb_d = dscr("vb", [S, 130], BF16)
    oT_d = dscr("oT", [8, 128, S], BF16)
    vbias_d = dscr("vbias", [8, VL], F32)
    lrow_d = nc.dram_tensor("lrow", [2, 1024], F32, kind="Internal").ap()
    rrow_d = nc.dram_tensor("rrow", [2, 1024], F32, kind="Internal").ap()
    arow_d = nc.dram_tensor("arow", [2, 512], F32, kind="Internal").ap()

    pe = Eng(nc, nc.tensor, "pe", es)
    act = Eng(nc, nc.scalar, "act", es)
    dve = Eng(nc, nc.vector, "dve", es)
    pool = Eng(nc, nc.gpsimd, "pool", es)
    sp = Eng(nc, nc.sync, "sp", es)

    def sb(name, shape, dt):
        return es.enter_context(nc.sbuf_tensor("g_" + name, list(shape), dt))

    ident = sb("ident", [128, 128], BF16)
    stats = sb("stats", [128, 2048], F32)
    stat_ptr = [0]

    def stat_cols(n):
        a = stat_ptr[0]
        stat_ptr[0] += n
        assert stat_ptr[0] <= 2048, "stats overflow"
        return stats[:, a:a + n]

    junk = sb("junk", [128, 1024], BF16)
    epsc = sb("epsc", [128, 1], F32)
    neglam = sb("neglam", [128, 1], F32)
    gsubc = sb("gsubc", [64, 1], F32)
    gq_t = sb("gq_t", [128, 64], F32)
    gk_t = sb("gk_t", [128, 64], F32)
    bconst = sb("bconst", [128, 16], F32)
    lamv = sb("lamv", [128, 4, 32], F32)
    lamt = sb("lamt", [128, 2, 32], F32)
    lams = sb("lams", [128, 4], F32)

    ds_const = DSem(nc, "ds_const", es)
    ds_cast = [DSem(nc, f"ds_cast{i}", es) for i in range(8)]

    def bcast_rows(src, n_part, n):
        return bass.AP(src.tensor, src.offset, [[0, n_part], [1, n]])

    tk_const = ds_const.issue(sp, [
        (ident[:], ident_d),
        (gq_t[:], bcast_rows(g_q, 128, 64)),
        (gk_t[:], bcast_rows(g_k, 128, 64)),
        (bconst[:, 0:8], bcast_rows(rel_bias[15:16, :], 128, 8)),
        (bconst[:, 8:16], bcast_rows(rel_bias[31:32, :], 128, 8)),
        (lamv[:, 0, :], bcast_rows(lam_in[0], 128, 32)),
        (lamv[:, 1, :], bcast_rows(lam_in[1], 128, 32)),
        (lamv[:, 2, :], bcast_rows(lam_in[2], 128, 32)),
        (lamv[:, 3, :], bcast_rows(lam_in[3], 128, 32)),
        (gsubc[:], bass.AP(g_sub.tensor, g_sub.offset, [[1, 64], [1, 1]])),
    ])
    tk_eps = dve.tok(dve.h.memset(epsc[:], EPS))
    dve.wait(tk_const)
    t1 = dve.tok(dve.h.tensor_tensor(out=lamt[:, 0, :], in0=lamv[:, 0, :], in1=lamv[:, 1, :], op=ALU.mult))
    t2 = dve.tok(dve.h.tensor_tensor(out=lamt[:, 1, :], in0=lamv[:, 2, :], in1=lamv[:, 3, :], op=ALU.mult))
    dve.wait(t1, t2)
    t3 = dve.tok(dve.h.tensor_reduce(out=lams[:, 0:2], in_=lamt[:], axis=AX.X, op=ALU.add))
    act.wait(t3)
    t4 = act.tok(act.h.activation(out=lams[:, 2:4], in_=lams[:, 0:2], func=AF.Exp))
    dve.wait(t4)
    t5 = dve.tok(dve.h.tensor_tensor(out=neglam[:], in0=lams[:, 3:4], in1=lams[:, 2:3], op=ALU.subtract))
    dve.wait(t5)
    t6 = dve.tok(dve.h.tensor_scalar(out=neglam[:], in0=neglam[:], scalar1=-LAMBDA_INIT, scalar2=None, op0=ALU.add))
    t7 = dve.tok(dve.h.tensor_scalar(out=gsubc[:], in0=gsubc[:], scalar1=(1.0 - LAMBDA_INIT), scalar2=None, op0=ALU.mult))
    t8 = dve.tok(dve.h.tensor_scalar(out=gq_t[:], in0=gq_t[:], scalar1=0.125, scalar2=None, op0=ALU.mult))
    bdiff = sb("bdiff", [128, 8], F32)
    cneg = sb("cneg", [128, 8], F32)
    t9 = dve.tok(dve.h.tensor_tensor(out=bdiff[:], in0=bconst[:, 8:16], in1=bconst[:, 0:8], op=ALU.subtract))
    act.wait(tk_const)
    t10 = act.tok(act.h.activation(out=cneg[:], in_=bconst[:, 0:8], func=AF.Exp, scale=-1.0))
    nbdiff = sb("nbdiff", [128, 8], F32)
    cneg31 = sb("cneg31", [128, 8], F32)
    t11 = dve.tok(dve.h.tensor_tensor(out=nbdiff[:], in0=bconst[:, 0:8], in1=bconst[:, 8:16], op=ALU.subtract))
    t12_ = act.tok(act.h.activation(out=cneg31[:], in_=bconst[:, 8:16], func=AF.Exp, scale=-1.0))
    tk_consts_ready = [tk_const, tk_eps, t6, t7, t8, t9, t10, t11, t12_]

    import os as _os
    _skip = _os.environ.get("KSKIP", "")

    def cast_dram(dsem, dst, src, ncols):
        if "cast" in _skip:
            return None
        pairs = []
        c0 = 0
        while c0 < ncols:
            c1 = min(ncols, c0 + 1408)
            pairs.append((dst[:, c0:c1], src[:, c0:c1]))
            c0 = c1
        return dsem.issue(pool, pairs)

    p0s = ExitStack()
    rb_sb = p0s.enter_context(nc.sbuf_tensor("z_rb", [32, 8], F32))
    ohm = p0s.enter_context(nc.sbuf_tensor("z_ohm", [32, VL], F32))
    vb_sb = p0s.enter_context(nc.sbuf_tensor("z_vb", [8, VL], F32))
    zps = p0s.enter_context(nc.psum_tensor("z_ps", [128, 1536], F32))
    ds_z0 = DSem(nc, "ds_z0", es)
    tk_z0 = ds_z0.issue(sp, [(rb_sb[:], rel_bias), (ohm[:], ohm_d)])
    pe.wait(tk_z0)
    for (a_, b_) in ((0, 512), (512, 1024), (1024, 1280)):
        ins = pe.h.matmul(zps[0:8, a_:b_], lhsT=rb_sb[:], rhs=ohm[:, a_:b_], start=True, stop=True)
    tz = pe.tok(ins)
    act.wait(tz)
    z1 = act.tok(act.h.activation(out=vb_sb[:, :], in_=zps[0:8, 0:VL], func=AF.Exp))
    tk_vst = ds_z0.issue(sp, [(vbias_d, vb_sb[:])], waits=[z1])
    p0s.close()
    for e_ in (pe, act, dve, pool, sp):
        e_.wait(tk_vst)
    tk_wg = [[], None]
    tk_wu = [[], None]
    for g_ in range(11):
        cs = slice(g_ * 256, (g_ + 1) * 256)
        tk_wg[0].append(ds_cast[0].issue(pool, [(wgb[0][g_].rearrange("p (cc f) -> cc p f", f=256), w_g[0][:, cs].rearrange("(cc p) f -> cc p f", p=128))]))
        tk_wu[0].append(ds_cast[1].issue(pool, [(wub[0][g_].rearrange("p (cc f) -> cc p f", f=256), w_u[0][:, cs].rearrange("(cc p) f -> cc p f", p=128))]))

    def token_phase(which):
        pes = ExitStack()

        def psb(name, shape, dt):
            return pes.enter_context(nc.sbuf_tensor(f"p{which}_{name}", list(shape), dt))

        def pps(name, shape, dt):
            return pes.enter_context(nc.psum_tensor(f"p{which}_{name}", list(shape), dt))

        wd = psb("wd", [128, NFC, D], BF16)
        xt = [psb("xt0", [128, 4, D], F32), psb("xt1", [128, 4, D], F32)]
        hT = [psb("hT0", [128, 8, 512], BF16), psb("hT1", [128, 8, 512], BF16)]
        actT = psb("actT", [128, NFC, 512], BF16)
        ws = [psb(f"ws{i}", [128, 2, 8, 256], BF16) for i in range(3)]
        gA = psb("gA", [128, D], F32)
        gB = psb("gB", [128, D], F32)
        hn = [psb("hn0", [128, D], BF16), psb("hn1", [128, D], BF16)]
        sg = [psb("sg0", [128, 512], BF16), psb("sg1", [128, 512], BF16)]
        tp = [pps("tp0", [128, 1024], BF16), pps("tp1", [128, 1024], BF16)]
        pg = [pps("pg0", [128, 512], F32), pps("pg1", [128, 512], F32)]
        pu = [pps("pu0", [128, 512], F32), pps("pu1", [128, 512], F32)]
        py = [pps("py0", [128, 512], F32), pps("py1", [128, 512], F32)]

        ds_x = [DSem(nc, f"p{which}_dsx{i}", pes) for i in range(2)]
        ds_xp = [DSem(nc, f"p{which}_dsxp{i}", pes) for i in range(2)]
        ds_ws = [DSem(nc, f"p{which}_dsw{i}", pes) for i in range(3)]
        ds_g = DSem(nc, f"p{which}_dsg", pes)
        ds_st = [DSem(nc, f"p{which}_dst{i}", pes) for i in range(4)]

        if which == 0:
            c2t = [psb("c2t0", [128, 4, 64], F32), psb("c2t1", [128, 4, 64], F32)]
            s2t = [psb("s2t0", [128, 4, 64], F32), psb("s2t1", [128, 4, 64], F32)]
            va_sb = psb("va_sb", [128, 4, 8, 65], BF16)
            vb_sb = psb("vb_sb", [128, 4, 2, 65], BF16)
            qaT_sb = psb("qaT_sb", [128, 4, 512], BF16)
            kaT_sb = psb("kaT_sb", [128, 4, 512], BF16)
            qbT_sb = psb("qbT_sb", [128, 4, 512], BF16)
            kbT_sb = psb("kbT_sb", [128, 512], BF16)
            qkf = [psb("qkf0", [128, 10, 64], F32), psb("qkf1", [128, 10, 64], F32)]
            qsq = psb("qsq", [128, 10, 64], F32)
            qrt = psb("qrt", [128, 10, 64], F32)
            qtm = psb("qtm", [128, 10, 64], F32)
            qkn = [psb(f"qkn{i}", [128, 640], BF16) for i in range(4)]
            ds_aux = [DSem(nc, f"p{which}_dsr{i}", pes) for i in range(2)]
            ds_auxp = [DSem(nc, f"p{which}_dsrp{i}", pes) for i in range(2)]
        else:
            wout = psb("wout", [128, 8, D], BF16)
            ot = [psb("ot0", [128, 8, 512], BF16), psb("ot1", [128, 8, 512], BF16)]
            ds_aux = [DSem(nc, f"p{which}_dso{i}", pes) for i in range(2)]
            ds_auxp = [DSem(nc, f"p{which}_dsop{i}", pes) for i in range(2)]

        st = K()
        if which == 1:
            for e_ in (pe, act, dve, pool, sp):
                e_.wait(P2_done)
        gsrcA = g_ffn[which]
        gsrcB = g_mix if which == 0 else g_fin
        tk_g = ds_g.issue(sp, [(gA[:], bcast_rows(gsrcA, 128, D)), (gB[:], bcast_rows(gsrcB, 128, D))])
        if which == 0:
            tk_wd = None if "wd" in _skip else ds_cast[2].issue(pool, [(wd[:], w_d[0].rearrange("(fc p) d -> p fc d", p=128))])
            for g_ in range(9):
                cs_ = slice(g_ * 256, (g_ + 1) * 256)
                tk_win = ds_cast[3].issue(pool, [(winb[g_].rearrange("p (cc f) -> cc p f", f=256), w_in[:, cs_].rearrange("(cc p) f -> cc p f", p=128))])
            bg_casts = []
            for g_ in range(11):
                cs_ = slice(g_ * 256, (g_ + 1) * 256)
                bg_casts.append(lambda cs_=cs_, g_=g_: tk_wg.__setitem__(1, ds_cast[4].issue(pool, [(wgb[1][g_].rearrange("p (cc f) -> cc p f", f=256), w_g[1][:, cs_].rearrange("(cc p) f -> cc p f", p=128))])))
                bg_casts.append(lambda cs_=cs_, g_=g_: tk_wu.__setitem__(1, ds_cast[5].issue(pool, [(wub[1][g_].rearrange("p (cc f) -> cc p f", f=256), w_u[1][:, cs_].rearrange("(cc p) f -> cc p f", p=128))])))
            for r_ in range(4):
                rs_ = slice(r_ * 704, (r_ + 1) * 704)
                bg_casts.append(lambda rs_=rs_: setattr(st, "tk_wd2", ds_cast[6].issue(pool, [(wd2b[rs_, :], w_d[1][rs_, :])])))
            for r_ in range(2):
                rs_ = slice(r_ * 512, (r_ + 1) * 512)
                bg_casts.append(lambda rs_=rs_: setattr(st, "tk_wout", ds_cast[7].issue(pool, [(woutb[rs_, :], w_out[rs_, :])])))
            bg_ptr = [0]

            def run_bg(frac):
                tgt = min(len(bg_casts), int(round(frac * len(bg_casts))))
                while bg_ptr[0] < tgt:
                    bg_casts[bg_ptr[0]]()
                    bg_ptr[0] += 1
            ones_tok = [pool.tok(pool.h.memset(va_sb[:, :, :, 64:65], 1.0)),
                        pool.tok(pool.h.memset(vb_sb[:, :, :, 64:65], 1.0))]
        else:
            ds_w3 = [DSem(nc, f"p{which}_dsw3{i}", pes) for i in range(2)]
            tk_wo = ds_w3[0].issue(sp, [(wout[:], woutb.rearrange("(j p) d -> p j d", p=128))],
                                     waits=[P1.tk_wout])

        loads = []
        for c in range(NCH):
            for g in range(11):
                loads.append(([(0, wgb[which][g]),
                               (1, wub[which][g])],
                              [tk_wg[which][g], tk_wu[which][g]] if which == 0 else [tk_wg[which], tk_wu[which]]))
            if which == 0:
                for (c0,) in ((1536,), (2048,), (0,), (512,), (1024,)):
                    prs = [(0, winb[c0 // 256])]
                    if c0 != 2048:
                        prs.append((1, winb[c0 // 256 + 1]))
                    loads.append((prs, [tk_win]))
        slot_loaded = {}
        slot_free = {}
        issued = [0]

        def issue_load(k):
            if k >= len(loads):
                return
            prs, waits = loads[k]
            s = k % 3
            pairs = [(ws[s][:, half, :, :].rearrange("p cc f -> p (cc f)"), src) for (half, src) in prs]
            slot_loaded[k] = ds_ws[s].issue(sp, pairs, waits=list(waits) + [slot_free.get(k - 3)])

        load_ptr = [0]

        def next_slot():
            k = load_ptr[0]
            load_ptr[0] += 1
            return k, ws[k % 3], slot_loaded[k]

        def release_slot(k, tok):
            slot_free[k] = tok
            issue_load(k + 3)

        x_src = x if which == 0 else x1s
        tk_x = {}
        xt_free = {}
        aux = {}

        def issue_x(c):
            if c >= NCH:
                return
            b = c % 2
            qx = sp if c < 2 else pool
            dsx_, dsa_ = (ds_x[b], ds_aux[b]) if c < 2 else (ds_xp[b], ds_auxp[b])
            tk_x[c] = dsx_.issue(qx, [(xt[b][:], x_src[c * 512:(c + 1) * 512, :].rearrange("(t p) d -> p t d", p=128))],
                                    waits=[xt_free.get(c - 2)])
            if which == 0:
                aux[c] = dsa_.issue(qx, [
                    (c2t[b][:], c2_d[c * 512:(c + 1) * 512, :].rearrange("(t p) d -> p t d", p=128)),
                    (s2t[b][:], s2_d[c * 512:(c + 1) * 512, :].rearrange("(t p) d -> p t d", p=128))],
                    waits=[xt_free.get(c - 2)])
            else:
                aux[c] = dsa_.issue(qx, [(ot[b][:], oT_d[:, :, c * 512:(c + 1) * 512].rearrange("j p f -> p j f"))],
                                       waits=[xt_free.get(c - 2), P2_done])

        st.hn_free = [None, None]
        st.tp_free = [None, None]
        st.hT_free = [None, None]
        st.pg_free = [None, None]
        st.pu_free = [None, None]
        st.sg_free = [None, None]
        st.py_free = [None, None]
        st.actT_free = None
        st.py_i = 0

        class NormT:
            def __init__(self, src_tiles, src_ready, gt, hb, bmap=(0, 1, 0, 1)):
                self.src, self.rdy, self.gt, self.hb = src_tiles, src_ready, gt, hb
                self.bmap = bmap
                self.cols = stat_cols(12)
                self.th = [None] * 4
                self.ready = None

            def pre(self, t):
                c_ = self.cols
                ssq, lnv, rstd = c_[:, t:t + 1], c_[:, 4 + t:5 + t], c_[:, 8 + t:9 + t]
                b = self.bmap[t]
                act.wait(self.rdy[t])
                t0_ = act.tok(act.h.activation(out=junk[:], in_=self.src[t], func=AF.Square, accum_out=ssq))
                act.wait(t0_, tk_eps)
                tl = act.tok(act.h.activation(out=lnv, in_=ssq, func=AF.Ln, bias=epsc[:, 0:1], scale=1.0 / D))
                act.wait(tl)
                tr = act.tok(act.h.activation(out=rstd, in_=lnv, func=AF.Exp, scale=-0.5))
                dve.wait(tr, self.rdy[t], st.hn_free[b], tk_g)
                self.th[t] = dve.tok(dve.h.scalar_tensor_tensor(out=hn[b][:], in0=self.src[t], scalar=rstd,
                                                                in1=self.gt[:], op0=ALU.mult, op1=ALU.mult))

            def post(self, t):
                b = self.bmap[t]
                pe.wait(self.th[t], st.tp_free[b], tk_const)
                for j in range(8):
                    ins = pe.h.transpose(out=tp[b][:, j * 128:(j + 1) * 128], in_=hn[b][:, j * 128:(j + 1) * 128],
                                         identity=ident[:])
                tt = pe.tok(ins)
                st.hn_free[b] = tt
                act.wait(tt, st.hT_free[self.hb])
                te = act.tok(act.h.activation(out=hT[self.hb][:, :, t * 128:(t + 1) * 128],
                                              in_=tp[b][:, :].rearrange("p (j k) -> p j k", k=128), func=AF.Copy))
                st.tp_free[b] = te
                self.ready = te

            def run_all(self):
                self.pre(0)
                self.pre(1)
                self.post(0)
                self.pre(2)
                self.post(1)
                self.pre(3)
                self.post(2)
                self.post(3)
                return self.ready

        def norm_T(src_tiles, src_ready, gt, hb):
            return NormT(src_tiles, src_ready, gt, hb).run_all()

        def gate_up(hb, hT_ready):
            act_ready = None
            for fc in range(NFC):
                fl = fc % 2
                if fl == 0:
                    k, slot, tk_l = next_slot()
                b = fc % 2
                pe.wait(tk_l, hT_ready, st.pg_free[b])
                for cc in range(8):
                    ins = pe.h.matmul(pg[b][:], lhsT=slot[:, 0, cc, fl * 128:(fl + 1) * 128], rhs=hT[hb][:, cc, :],
                                      start=(cc == 0), stop=(cc == 7))
                tg = pe.tok(ins)
                pe.wait(st.pu_free[b])
                for cc in range(8):
                    ins = pe.h.matmul(pu[b][:], lhsT=slot[:, 1, cc, fl * 128:(fl + 1) * 128], rhs=hT[hb][:, cc, :],
                                      start=(cc == 0), stop=(cc == 7))
                tu = pe.tok(ins)
                if fl == 1:
                    release_slot(k, tu)
                act.wait(tg, st.sg_free[b])
                ts = act.tok(act.h.activation(out=sg[b][:], in_=pg[b][:], func=AF.Silu))
                st.pg_free[b] = ts
                dve.wait(ts, tu, st.actT_free)
                tm = dve.tok(dve.h.tensor_tensor(out=actT[:, fc, :], in0=sg[b][:], in1=pu[b][:], op=ALU.mult))
                st.pu_free[b] = tm
                st.sg_free[b] = tm
                act_ready = tm
                st.hT_free[hb] = tu
            return act_ready

        def down(xb, act_ready, cb=None):
            ready = []
            for t in range(4):
                if cb is not None and t > 0:
                    cb(t - 1, ready)
                for n in range(2):
                    b = st.py_i % 2
                    st.py_i += 1
                    pe.wait(act_ready, st.py_free[b], tk_wd)
                    for fc in range(NFC):
                        ins = pe.h.matmul(py[b][:], lhsT=actT[:, fc, t * 128:(t + 1) * 128],
                                          rhs=wd[:, fc, n * 512:(n + 1) * 512], start=(fc == 0), stop=(fc == NFC - 1))
                    ty = pe.tok(ins)
                    dve.wait(ty)
                    xs = xt[xb][:, t, n * 512:(n + 1) * 512]
                    tr_ = dve.tok(dve.h.scalar_tensor_tensor(out=xs, in0=py[b][:], scalar=0.5, in1=xs,
                                                            op0=ALU.mult, op1=ALU.add))
                    st.py_free[b] = tr_
                st.actT_free = ty
                ready.append(tr_)
            if cb is not None:
                cb(3, ready)
            return ready

        class StageA:
            def __init__(self, c, bmap=(0, 1, 0, 1)):
                self.c = c
                self.bmap = bmap
                self.xb = c % 2
                self.tiles = [xt[self.xb][:, t, :] for t in range(4)]
                self.rdy = [tk_x[c]] * 4
                self.nt = None

            def wout_tile(self, t):
                c, xb = self.c, self.xb
                for n in range(2):
                    b = st.py_i % 2
                    st.py_i += 1
                    pe.wait(aux[c], tk_wo, st.py_free[b])
                    for j in range(8):
                        ins = pe.h.matmul(py[b][:], lhsT=ot[xb][:, j, t * 128:(t + 1) * 128],
                                          rhs=wout[:, j, n * 512:(n + 1) * 512], start=(j == 0), stop=(j == 7))
                    ty = pe.tok(ins)
                    dve.wait(ty, tk_x[c])
                    xs = xt[xb][:, t, n * 512:(n + 1) * 512]
                    tr_ = dve.tok(dve.h.tensor_tensor(out=xs, in0=py[b][:], in1=xs, op=ALU.add))
                    st.py_free[b] = tr_
                self.rdy[t] = tr_
                st.ot_done = ty

            def step(self, k):
                if k == 0:
                    if which == 1:
                        self.rdy = [None] * 4
                        self.wout_tile(0)
                        self.wout_tile(1)
                    self.nt = NormT(self.tiles, self.rdy, gA, 0, self.bmap)
                    self.nt.pre(0)
                    self.nt.pre(1)
                elif k == 1:
                    if which == 1:
                        self.wout_tile(2)
                        self.wout_tile(3)
                    self.nt.post(0)
                    self.nt.pre(2)
                elif k == 2:
                    self.nt.post(1)
                    self.nt.pre(3)
                else:
                    self.nt.post(2)
                    self.nt.post(3)

            def run_all(self):
                for k in range(4):
                    self.step(k)
                return self.nt.ready

        def stage_A(c):
            return StageA(c).run_all()

        def evac_act(dst, src, waits):
            act.wait(*waits)
            return act.tok(act.h.activation(out=dst, in_=src, func=AF.Copy))

        def stage_D1(c, x1_ready):
            xb = c % 2
            tiles = [xt[xb][:, t, :] for t in range(4)]
            ck(f"p0_A{c+1}")
            if getattr(st, "n2", None) is not None:
                h2_ready = st.n2.ready
            else:
                h2_ready = norm_T(tiles, x1_ready, gB, 1)
            ck(f"p0_W{c}a")
            tk_x1st = ds_st[0].issue(pool, [(x1s[c * 512:(c + 1) * 512, :].rearrange("(t p) d -> p t d", p=128), xt[xb][:])],
                                     waits=[x1_ready[3]])
            stage_done = [tk_x1st]
            kD, slotD, tkD = next_slot()
            kE, slotE, tkE = next_slot()
            qk_ready = [[None, None, None] for _ in range(4)]
            ty_last = [None]
            tv_last = [None]

            def qk_mm(t):
                qb_ = t % 2
                for half in range(2):
                    b = st.py_i % 2
                    st.py_i += 1
                    pe.wait(tkD, h2_ready, st.py_free[b])
                    for cc in range(8):
                        ins = pe.h.matmul(py[b][:, 0:256], lhsT=hT[1][:, cc, t * 128:(t + 1) * 128],
                                          rhs=slotD[:, half, cc, :], start=(cc == 0), stop=(cc == 7))
                    ty = pe.tok(ins)
                    te = evac_act(qkf[qb_][:, half * 4:(half + 1) * 4, :],
                                  py[b][:, 0:256].rearrange("p (h d) -> p h d", d=64), [ty, st.qkf_free[qb_]])
                    st.py_free[b] = te
                    qk_ready[t][half] = te
                b = st.py_i % 2
                st.py_i += 1
                pe.wait(tkE, st.py_free[b])
                for cc in range(8):
                    ins = pe.h.matmul(py[b][:, 0:256], lhsT=hT[1][:, cc, t * 128:(t + 1) * 128],
                                      rhs=slotE[:, 0, cc, :], start=(cc == 0), stop=(cc == 7))
                ty = pe.tok(ins)
                te = evac_act(qkf[qb_][:, 8:10, :], py[b][:, 0:128].rearrange("p (h d) -> p h d", d=64), [ty])
                qk_ready[t][2] = te
                tv = evac_act(vb_sb[:, t, :, 0:64], py[b][:, 128:256].rearrange("p (h d) -> p h d", d=64),
                              [st.vb_free, ones_tok])
                st.py_free[b] = tv
                ty_last[0] = ty
                tv_last[0] = tv

            def qk_chain(t):
                qb_ = t % 2
                f = qkf[qb_]
                dve.wait(qk_ready[t])
                a1 = dve.tok(dve.h.tensor_tensor(out=qsq[:], in0=f[:], in1=f[:], op=ALU.mult))
                hs = stat_cols(10)
                hl = stat_cols(10)
                hr = stat_cols(10)
                dve.wait(a1)
                a2 = dve.tok(dve.h.tensor_reduce(out=hs, in_=qsq[:], axis=AX.X, op=ALU.add))
                act.wait(a2, tk_eps)
                a3 = act.tok(act.h.activation(out=hl, in_=hs, func=AF.Ln, bias=epsc[:, 0:1], scale=1.0 / 64))
                act.wait(a3)
                a4 = act.tok(act.h.activation(out=hr, in_=hl, func=AF.Exp, scale=-0.5))
                dve.wait(a4)
                a5 = dve.tok(dve.h.tensor_tensor(out=qrt[:], in0=f[:], in1=hr.unsqueeze(2).to_broadcast([128, 10, 64]),
                                                 op=ALU.mult))
                dve.wait(a5, tk_consts_ready)
                a6 = dve.tok(dve.h.tensor_tensor(out=qrt[:, 0:8, :], in0=qrt[:, 0:8, :],
                                                 in1=gq_t[:].unsqueeze(1).to_broadcast([128, 8, 64]), op=ALU.mult))
                a7 = dve.tok(dve.h.tensor_tensor(out=qrt[:, 8:10, :], in0=qrt[:, 8:10, :],
                                                 in1=gk_t[:].unsqueeze(1).to_broadcast([128, 2, 64]), op=ALU.mult))
                st.qkf_free[qb_] = a5
                dve.wait(a6, a7, aux[c])
                qv = qrt[:].rearrange("p h (i two) -> p h i two", two=2)
                tv_ = qtm[:].rearrange("p h (i two) -> p h i two", two=2)
                sv = s2t[xb][:, t, :].rearrange("p (i two) -> p i two", two=2)
                a8 = dve.tok(dve.h.tensor_tensor(out=tv_[:, :, :, 0], in0=qv[:, :, :, 1],
                                                 in1=sv[:, :, 0].unsqueeze(1).to_broadcast([128, 10, 32]), op=ALU.mult))
                a9 = dve.tok(dve.h.tensor_tensor(out=tv_[:, :, :, 1], in0=qv[:, :, :, 0],
                                                 in1=sv[:, :, 1].unsqueeze(1).to_broadcast([128, 10, 32]), op=ALU.mult))
                dve.wait(a8, a9)
                a10 = dve.tok(dve.h.tensor_tensor(out=qrt[:], in0=qrt[:],
                                                  in1=c2t[xb][:, t, :].unsqueeze(1).to_broadcast([128, 10, 64]), op=ALU.mult))
                dve.wait(a8, a9, a10, st.qkn_free[t])
                a11 = dve.tok(dve.h.tensor_tensor(
                    out=qkn[t][:, 0:512].rearrange("p (j hh d) -> p hh j d", hh=2, d=64),
                    in0=qrt[:, 0:8, :].rearrange("p (hh j) d -> p hh j d", hh=2),
                    in1=qtm[:, 0:8, :].rearrange("p (hh j) d -> p hh j d", hh=2), op=ALU.add))
                a12 = dve.tok(dve.h.tensor_tensor(out=qkn[t][:, 512:640].rearrange("p (h d) -> p h d", d=64),
                                                  in0=qrt[:, 8:10, :], in1=qtm[:, 8:10, :], op=ALU.add))
                st.qkn_ready[t] = [a11, a12]

            def feat_slot(dst_sb, nm):
                kk, slot, tkl = next_slot()
                for half in range(2):
                    for jj in range(2):
                        j = half * 2 + jj
                        b = st.py_i % 2
                        st.py_i += 1
                        pe.wait(tkl, st.py_free[b])
                        for cc in range(8):
                            ins = pe.h.matmul(py[b][:], lhsT=slot[:, half, cc, jj * 128:(jj + 1) * 128],
                                              rhs=hT[1][:, cc, :], start=(cc == 0), stop=(cc == 7))
                        ty = pe.tok(ins)
                        te = evac_act(dst_sb[:, j, :], py[b][:], [ty, st.stg_free.get(nm)])
                        st.py_free[b] = te
                release_slot(kk, ty)
                st.stg_ready[nm] = te

            qk_mm(0)
            qk_mm(1)
            qk_chain(0)
            qk_mm(2)
            qk_chain(1)
            qk_mm(3)
            release_slot(kD, ty_last[0])
            release_slot(kE, ty_last[0])
            tk_vb_ready = tv_last[0]
            feat_slot(qaT_sb, "qa")
            qk_chain(2)
            feat_slot(kaT_sb, "ka")
            qk_chain(3)
            kC, slotC, tkC = next_slot()
            for t in range(4):
                for half in range(2):
                    b = st.py_i % 2
                    st.py_i += 1
                    pe.wait(tkC, st.py_free[b])
                    for cc in range(8):
                        ins = pe.h.matmul(py[b][:, 0:256], lhsT=hT[1][:, cc, t * 128:(t + 1) * 128],
                                          rhs=slotC[:, half, cc, :], start=(cc == 0), stop=(cc == 7))
                    ty = pe.tok(ins)
                    te = evac_act(va_sb[:, t, half * 4:(half + 1) * 4, 0:64],
                                  py[b][:, 0:256].rearrange("p (h d) -> p h d", d=64), [ty, st.va_free, ones_tok])
                    st.py_free[b] = te
            release_slot(kC, ty)
            st.hT_free[1] = ty
            tk_va_ready = te
            ck(f"p0_W{c}d")
            for t in range(4):
                qb_ = t % 2
                b = t % 2
                pe.wait(st.qkn_ready[t], st.tp_free[b])
                for j in range(5):
                    ins = pe.h.transpose(out=tp[b][:, j * 128:(j + 1) * 128], in_=qkn[t][:, j * 128:(j + 1) * 128],
                                         identity=ident[:])
                tt = pe.tok(ins)
                st.qkn_free[t] = tt
                dve.wait(tt, st.stg_free.get("qb"))
                e1 = dve.tok(dve.h.tensor_copy(out=qbT_sb[:, :, t * 128:(t + 1) * 128],
                                               in_=tp[b][:, 0:512].rearrange("p (j k) -> p j k", k=128)))
                e2 = dve.tok(dve.h.tensor_copy(out=kbT_sb[:, t * 128:(t + 1) * 128], in_=tp[b][:, 512:640]))
                st.tp_free[b] = [e1, e2]
            ck(f"p0_W{c}e")
            sl = slice(c * 512, (c + 1) * 512)
            s1 = ds_st[1].issue(pool, [
                (qaT_d[:, :, sl].rearrange("j p f -> p j f"), qaT_sb[:]),
                (kaT_d[:, :, sl].rearrange("j p f -> p j f"), kaT_sb[:])],
                waits=[st.stg_ready["qa"], st.stg_ready["ka"]])
            s2 = ds_st[2].issue(pool, [
                (va_d[sl, :].rearrange("(t p) e -> p t e", p=128), va_sb[:].rearrange("p t h e -> p t (h e)")),
                (vb_d[sl, :].rearrange("(t p) e -> p t e", p=128), vb_sb[:].rearrange("p t h e -> p t (h e)"))],
                waits=[tk_va_ready, tk_vb_ready])
            s3 = ds_st[3].issue(pool, [
                (qbT_d[:, :, sl].rearrange("j p f -> p j f"), qbT_sb[:]),
                (kbT_d[:, sl], kbT_sb[:])],
                waits=[e1, e2])
            st.stg_free["qa"] = s1
            st.stg_free["ka"] = s1
            st.va_free = s2
            st.vb_free = s2
            st.stg_free["qb"] = s3
            xt_free[c] = [tk_x1st, h2_ready, s3]
            st.all_stores = [tk_x1st, s1, s2, s3]

        def stage_F3(c, x3_ready):
            xb = c % 2
            ssq = stat_cols(4)
            lnv = stat_cols(4)
            rstd = stat_cols(4)
            tks = []
            for t in range(4):
                act.wait(x3_ready[t])
                if tks:
                    act.wait(tks[-1])
                tks.append(act.tok(act.h.activation(out=junk[:], in_=xt[xb][:, t, :], func=AF.Square,
                                                    accum_out=ssq[:, t:t + 1])))
            act.wait(tks[-1], tk_eps)
            tl = act.tok(act.h.activation(out=lnv, in_=ssq, func=AF.Ln, bias=epsc[:, 0:1], scale=1.0 / D))
            act.wait(tl)
            tr = act.tok(act.h.activation(out=rstd, in_=lnv, func=AF.Exp, scale=-0.5))
            for t in range(4):
                dve.wait(tr, x3_ready[t], tk_g)
                tn = dve.tok(dve.h.scalar_tensor_tensor(out=xt[xb][:, t, :], in0=xt[xb][:, t, :], scalar=rstd[:, t:t + 1],
                                                        in1=gB[:], op0=ALU.mult, op1=ALU.mult))
            tso = ds_st[0].issue(pool, [(out[c * 512:(c + 1) * 512, :].rearrange("(t p) d -> p t d", p=128), xt[xb][:])],
                                 waits=[tn])
            xt_free[c] = [tso]
            st.all_stores = [tso]

        if which == 0:
            st.qkf_free = [None, None]
            st.qkn_free = [None] * 4
            st.qkn_ready = [None] * 4
            st.stg_free = {}
            st.stg_ready = {}
            st.va_free = None
            st.vb_free = None

        ck(f"p{which}_prologue")
        issue_x(0)
        for k in range(3):
            issue_load(k)
        if which == 1:
            tk_wd = ds_w3[1].issue(sp, [(wd[:], wd2b.rearrange("(fc p) d -> p fc d", p=128))],
                                     waits=[P1.tk_wd2])
        issue_x(1)
        ck(f"p{which}_loads")
        hT_ready = stage_A(0)
        ck(f"p{which}_A0")
        for c in range(NCH):
            act_ready = gate_up(0, hT_ready)
            ck(f"p{which}_G{c}")
            if which == 0 and c >= 1:
                run_bg((c + 0.4) / NCH)
            nxt = StageA(c + 1, (0, 1, 0, 1) if (which == 1 or NCH == 1) else (0, 1, 1, 1)) if c + 1 < NCH else None
            n2 = [None]

            def d_cb(k, rdy, c=c, nxt=nxt, n2=n2):
                if which == 1 or NCH == 1:
                    if nxt is not None:
                        nxt.step(k)
                    return
                if k == 0:
                    if nxt is not None:
                        nxt.step(0)
                    n2[0] = NormT([xt[c % 2][:, t, :] for t in range(4)], rdy, gB, 1, (0, 0, 0, 1))
                elif k == 1:
                    if nxt is not None:
                        nxt.nt.post(0)
                        nxt.nt.post(1)
                    n2[0].pre(0)
                    if nxt is not None:
                        nxt.nt.pre(2)
                elif k == 2:
                    n2[0].post(0)
                    if nxt is not None:
                        nxt.nt.post(2)
                    n2[0].pre(1)
                    if nxt is not None:
                        nxt.nt.pre(3)
                else:
                    n2[0].post(1)
                    if nxt is not None:
                        nxt.nt.post(3)
                    n2[0].pre(2)
                    n2[0].pre(3)
                    pe.wait(st.pg_free[0])
                    for _d in range(16):
                        kw_ = pe.h.matmul(pg[0][:], lhsT=actT[:, 0, 0:128], rhs=actT[:, 1, :], start=True, stop=True)
                    st.actT_free = [st.actT_free, pe.tok(kw_)]
                    n2[0].post(2)
                    n2[0].post(3)

            ready = down(c % 2, act_ready, cb=d_cb)
            st.n2 = n2[0]
            ck(f"p{which}_D{c}")
            if which == 0 and c >= 1:
                run_bg((c + 0.7) / NCH)
            if which == 0:
                if nxt is not None:
                    hT_ready = nxt.nt.ready
                stage_D1(c, ready)
                ck(f"p{which}_W{c}")
                run_bg((c + 1.0) / NCH)
            else:
                if nxt is not None:
                    hT_ready = nxt.nt.ready
                stage_F3(c, ready)
            issue_x(c + 2)
        fin = list(st.all_stores)
        for e in (pe, act, dve):
            pass
        st.final = fin
        st.exit = pes
        return st

    def attention_phase():
        pes = ExitStack()

        def psb(name, shape, dt):
            return pes.enter_context(nc.sbuf_tensor("a_" + name, list(shape), dt))

        def pps(name, shape, dt):
            return pes.enter_context(nc.psum_tensor("a_" + name, list(shape), dt))

        kaT = psb("kaT", [128, 4, S], BF16)
        kbT = psb("kbT", [128, S], BF16)
        var = psb("var", [128, NT, 520], BF16)
        vbr = psb("vbr", [128, NT, 130], BF16)
        Z = psb("Z", [128, 8, ZW], F32)
        qa = [psb("qa0", [128, 4, 512], BF16), psb("qa1", [128, 4, 512], BF16)]
        qb = [psb("qb0", [128, 4, 512], BF16), psb("qb1", [128, 4, 512], BF16)]
        NPT = 6
        pT = [psb(f"pT{i}", [128, 1024], BF16) for i in range(NPT)]
        pTf = [psb(f"pTf{i}", [128, 1024], F32) for i in range(2)]
        accs = psb("accs", [65, 1024], F32)
        Rr = psb("Rr", [64, 1024], F32)
        t12 = psb("t12", [64, 1024], F32)
        o_ = psb("o_", [64, 512], F32)
        sq_ = psb("sq_", [64, 512], BF16)
        at4l = [psb("at4l0", [128, 4], F32), psb("at4l1", [128, 4], F32)]
        at4 = [psb("at40", [128, 4], F32), psb("at41", [128, 4], F32)]
        lt_free2 = [None, None]
        alp = psb("alp", [64, 512], F32)
        fin = [psb("fin0", [64, 1024], BF16), psb("fin1", [64, 1024], BF16)]
        ones64 = psb("ones64", [64, 64], BF16)
        jmat = psb("jmat", [128, 128], F32)
        zt = psb("zt", [128, 512], BF16)
        lt = [psb("lt0", [128, 8], F32), psb("lt1", [128, 8], F32)]
        ltr = [psb("ltr0", [128, 8], F32), psb("ltr1", [128, 8], F32)]
        lt_free = [None, None]

        stp = [pps("st0", [128, 1024], F32), pps("st1", [128, 1024], F32)]
        accP = [pps("accA", [128, 1024], F32), pps("accB", [128, 1024], F32)]

        ds_kv = DSem(nc, "a_dskv", pes)
        ds_q = [DSem(nc, f"a_dsq{i}", pes) for i in range(2)]
        ds_z = DSem(nc, "a_dsz", pes)
        ds_z2 = DSem(nc, "a_dsz2", pes)
        ds_o = [DSem(nc, f"a_dso{i}", pes) for i in range(2)]
        ds_e = [DSem(nc, f"a_dse{i}", pes) for i in range(4)]
        ds_e2 = [DSem(nc, f"a_dsf{i}", pes) for i in range(2)]

        p1_done = P1.final
        for e_ in (pe, act, dve, pool, sp):
            e_.wait(p1_done)
        NKV = 4 if NT % 4 == 0 else 1
        KTP = NT // NKV
        ds_kvp = [DSem(nc, f"a_dskv{i}", pes) for i in range(NKV)]
        tk_kvp = []

        m3 = dve.tok(dve.h.memset(ones64[:], 1.0))
        m4 = dve.tok(dve.h.memset(zt[:], 0.0))
        tk_sel = [m3, m4]
        st_free0 = None
        def load_kv_piece(i_):
            ts_ = slice(i_ * KTP * 128, (i_ + 1) * KTP * 128)
            kk_ = slice(i_ * KTP, (i_ + 1) * KTP)
            tk_kvp.append(ds_kvp[i_].issue(sp, [
                (kaT[:, :, ts_], kaT_d[:, :, ts_].rearrange("j p s -> p j s")),
                (kbT[:, ts_], kbT_d[:, ts_]),
                (var[:, kk_, :], va_d[ts_, :].rearrange("(kt p) e -> p kt e", p=128)),
                (vbr[:, kk_, :], vb_d[ts_, :].rearrange("(kt p) e -> p kt e", p=128))], waits=p1_done))

        tk_q = {}
        q_free = {}

        def issue_q(qc):
            if qc >= NCH:
                return
            b = qc % 2
            sl = slice(qc * 512, (qc + 1) * 512)
            tk_q[qc] = ds_q[b].issue(sp, [
                (qa[b][:], qaT_d[:, :, sl].rearrange("j p f -> p j f")),
                (qb[b][:], qbT_d[:, :, sl].rearrange("j p f -> p j f"))], waits=list(p1_done) + [q_free.get(qc - 2)])

        issue_q(0)
        load_kv_piece(0)
        tk_zh = ds_z2.issue(sp, [(Z[:], bass.AP(vbias_d.tensor, vbias_d.offset, [[1, 128], [VL, 8], [1, ZW]])),
                                 (jmat[:], jmat_d)], waits=[tk_vst])
        for i_ in range(1, NKV):
            load_kv_piece(i_)
        issue_q(1)
        zi = 0
        zfree = [None, None]
        for hh in range(8):
            for (a_, b_) in ((0, 512), (512, 1024), (1024, ZW)):
                zb = stp[zi % 2]
                pe.wait(tk_zh, zfree[zi % 2])
                tzz = pe.tok(pe.h.matmul(zb[:, 0:b_ - a_], lhsT=jmat[:], rhs=Z[:, hh, a_:b_], start=True, stop=True))
                dve.wait(tzz)
                tk_Z = dve.tok(dve.h.tensor_copy(out=Z[:, hh, a_:b_], in_=zb[:, 0:b_ - a_]))
                zfree[zi % 2] = tk_Z
                zi += 1
        st_free0 = [zfree[0], zfree[1]]
        tk_EZ = tk_Z

        units = []
        grp = 0
        for qc in range(NCH):
            near = [kt for kt in range(NT) if -256 < kt * 128 - qc * 512 < 640]
            far = [kt for kt in range(NT) if kt not in near]
            order = []
            step = max(1, len(far) // max(1, len(near)))
            ni = 0
            for fi, kt in enumerate(far):
                order.append(kt)
                if (fi + 1) % step == 0 and ni < len(near):
                    order.append(near[ni])
                    ni += 1
            order.extend(near[ni:])
            assert sorted(order) == list(range(NT))
            for h in range(8):
                for oi, kt in enumerate(order):
                    units.append(("A", qc, h, kt, grp, oi == 0, oi == NT - 1))
                grp += 1
            for j in range(4):
                for kt in range(NT):
                    units.append(("B", qc, j, kt, grp, kt == 0, kt == NT - 1))
                grp += 1

        st_free = [st_free0, st_free0]
        pT_free = [None] * NPT
        exp_done = {}
        near_i = [0]
        pTf_free = [None, None]
        pair_free = [[], []]
        tmp_free = {}
        pending = []
        o_stores = []
        fin_free = [None, None]
        fin_i = [0]
        last_pv = [None]

        def emit_qk(i):
            kind, qc, hj, kt = units[i][:4]
            s = i % 2
            qbuf = qc % 2
            pe.wait(st_free[s], tk_kvp[kt // KTP], tk_q[qc], tk_sel)
            ks = slice(kt * 128, (kt + 1) * 128)
            for _d in range(N_DUMMY):
                pe.h.matmul(stp[s][:, 0:DUMMY_W], lhsT=zt[:, 0:128], rhs=zt[:, 0:DUMMY_W], start=True, stop=True)
            if kind == "A":
                jc, rb = hj // 2, (hj % 2) * 64
                for m in range(2):
                    r0 = rb + 32 * m
                    ins = pe.h.matmul(stp[s][:, m * 512:(m + 1) * 512], lhsT=kaT[r0:r0 + 32, jc, ks],
                                      rhs=qa[qbuf][r0:r0 + 32, jc, :], start=True, stop=True, tile_position=(r0, 0))
            else:
                for m in range(2):
                    r0 = 64 * m
                    ins = pe.h.matmul(stp[s][:, m * 512:(m + 1) * 512], lhsT=kbT[r0:r0 + 64, ks],
                                      rhs=qb[qbuf][r0:r0 + 64, hj, :], start=True, stop=True, tile_position=(r0, 0))
            return pe.tok(ins)

        def emit_exp(i, tk_qk):
            kind, qc, hj, kt = units[i][:4]
            s = i % 2
            ps = i % NPT
            if kind == "A":
                delta = kt * 128 - qc * 512
                shift_left = max(0, 4 * qc - 1) >= max(0, NT - (4 * qc + 5))
                cn_ = cneg if shift_left else cneg31
                if -256 < delta < 640:
                    x0 = 512 - delta
                    nb = near_i[0] % 2
                    near_i[0] += 1
                    act.wait(tk_qk, pTf_free[nb])
                    te = act.tok(act.h.activation(out=pTf[nb][:], in_=stp[s][:], func=AF.Exp, scale=SC_A))
                    st_free[s] = te
                    dve.wait(te, tk_EZ, pT_free[ps], tk_consts_ready)
                    td = dve.tok(dve.h.scalar_tensor_tensor(
                        out=pT[ps][:, :].rearrange("p (m f) -> p m f", m=2),
                        in0=pTf[nb][:, :].rearrange("p (m f) -> p m f", m=2), scalar=cn_[:, hj:hj + 1],
                        in1=Z[:, hj, x0:x0 + 512].unsqueeze(1).to_broadcast([128, 2, 512]),
                        op0=ALU.mult, op1=ALU.mult))
                    pTf_free[nb] = td
                    exp_done[i] = td
                    return
                else:
                    act.wait(tk_qk, pT_free[ps], tk_consts_ready)
                    if (delta < 0) == shift_left:
                        ins = act.h.activation(out=pT[ps][:], in_=stp[s][:], func=AF.Exp, scale=SC_A)
                    else:
                        bd_ = bdiff if shift_left else nbdiff
                        ins = act.h.activation(out=pT[ps][:], in_=stp[s][:], func=AF.Exp, bias=bd_[:, hj:hj + 1],
                                               scale=SC_A)
            else:
                act.wait(tk_qk, pT_free[ps])
                ins = act.h.activation(out=pT[ps][:], in_=stp[s][:], func=AF.Exp)
            te = act.tok(ins)
            st_free[s] = te
            exp_done[i] = te

        def emit_pv(i):
            kind, qc, hj, kt, g, first, last = units[i]
            ps = i % NPT
            acc = accP[g % 2]
            pe.wait(exp_done.pop(i))
            if first:
                pe.wait(pair_free[g % 2])
            for m in range(2):
                if kind == "A":
                    lh = var[:, kt, hj * 65:(hj + 1) * 65]
                else:
                    lh = vbr[:, kt, m * 65:(m + 1) * 65]
                ins = pe.h.matmul(acc[0:65, m * 512:(m + 1) * 512], lhsT=lh, rhs=pT[ps][:, m * 512:(m + 1) * 512],
                                  start=first, stop=last)
            tp_ = pe.tok(ins)
            pT_free[ps] = tp_
            last_pv[0] = tp_
            if last:
                schedule_epilogue(i, kind, qc, hj, tp_, g)

        def schedule_epilogue(i, kind, qc, hj, tk_pv, g):
            sv = {}
            acc = accP[g % 2]
            ep = [acc[:, 0:512], acc[:, 512:1024]]
            ep_free = [None, None]
            pair_free[g % 2] = []

            par = g % 2

            def e1():
                dve.wait(tk_pv, tmp_free.get("accs"))
                sv["c"] = dve.tok(dve.h.tensor_copy(out=accs[:], in_=acc[0:65, :]))
                pair_free[g % 2].append(sv["c"])
                d1 = ds_e[0].issue(sp, [(lrow_d[par:par + 1, :], accs[64:65, :])], waits=[sv["c"]])
                sv["d2"] = ds_e[1].issue(sp, [(lt[par][:], lrow_d[par, :].rearrange("(p f) -> p f", f=8))],
                                         waits=[d1, lt_free[par]])

            def e2():
                dve.wait(sv["d2"])
                v1 = dve.tok(dve.h.reciprocal(out=ltr[par][:], in_=lt[par][:]))
                lt_free[par] = v1
                d3 = ds_e[2].issue(sp, [(rrow_d[par, :].rearrange("(p f) -> p f", f=8), ltr[par][:])], waits=[v1])
                sv["d4"] = ds_e[3].issue(sp, [(Rr[:], bass.AP(rrow_d.tensor, rrow_d.offset + par * 1024,
                                                               [[0, 64], [1, 1024]]))],
                                         waits=[d3, tmp_free.get("Rr")])

            def e3():
                dve.wait(sv["d4"])
                if kind == "A":
                    dve.wait(tmp_free.get("t12"))
                    m_ = dve.tok(dve.h.tensor_tensor(out=t12[:], in0=accs[0:64, :], in1=Rr[:], op=ALU.mult))
                    tmp_free["accs"] = m_
                    tmp_free["Rr"] = m_
                    dve.wait(m_, tmp_free.get("o_"), tk_consts_ready)
                    oo = dve.tok(dve.h.scalar_tensor_tensor(out=o_[:], in0=t12[:, 512:1024], scalar=neglam[0:64, 0:1],
                                                            in1=t12[:, 0:512], op0=ALU.mult, op1=ALU.add))
                    tmp_free["t12"] = oo
                    dve.wait(oo, tmp_free.get("sq_"))
                    sv["sq"] = dve.tok(dve.h.tensor_tensor(out=sq_[:], in0=o_[:], in1=o_[:], op=ALU.mult))
                else:
                    fb = fin_i[0] % 2
                    fin_i[0] += 1
                    sv["fb"] = fb
                    dve.wait(fin_free[fb])
                    m_ = dve.tok(dve.h.tensor_tensor(out=fin[fb][:], in0=accs[0:64, :], in1=Rr[:], op=ALU.mult))
                    tmp_free["accs"] = m_
                    tmp_free["Rr"] = m_
                    sl = slice(qc * 512, (qc + 1) * 512)
                    c1, c2 = 4 + hj // 2, 6 + hj // 2
                    po = (hj % 2) * 64
                    tko = ds_o[fb].issue(pool, [(oT_d[c1, po:po + 64, sl], fin[fb][:, 0:512]),
                                                (oT_d[c2, po:po + 64, sl], fin[fb][:, 512:1024])], waits=[m_])
                    fin_free[fb] = tko
                    o_stores.append(tko)

            def e4():
                pe.wait(sv["sq"], sv["c"], tk_sel)
                sqv = sq_[:, :].rearrange("d (p f) -> d f p", f=4)
                for tb in range(4):
                    ins = pe.h.matmul(ep[0][:, tb:tb + 1], lhsT=sqv[:, tb, :], rhs=ones64[:, 0:1], start=True, stop=True)
                sv["ss"] = pe.tok(ins)

            def e5():
                act.wait(sv["ss"], lt_free2[par], tk_eps)
                l_ = act.tok(act.h.activation(out=at4l[par][:], in_=ep[0][:, 0:4], func=AF.Ln, bias=epsc[:, 0:1],
                                              scale=1.0 / 64))
                ep_free[0] = l_
                pair_free[g % 2].append(l_)
                tmp_free["sq_"] = sv["ss"]
                act.wait(l_)
                a_ = act.tok(act.h.activation(out=at4[par][:], in_=at4l[par][:], func=AF.Exp, scale=-0.5))
                d5 = ds_e2[0].issue(sp, [(arow_d[par, :].rearrange("(p f) -> p f", f=4), at4[par][:])], waits=[a_])
                lt_free2[par] = d5
                sv["al"] = ds_e2[1].issue(sp, [(alp[:], bass.AP(arow_d.tensor, arow_d.offset + par * 512,
                                                                [[0, 64], [1, 512]]))],
                                          waits=[d5, tmp_free.get("alp")])

            def e6():
                fb = fin_i[0] % 2
                fin_i[0] += 1
                dve.wait(sv["al"], fin_free[fb], tk_consts_ready)
                f_ = dve.tok(dve.h.scalar_tensor_tensor(out=fin[fb][:, 0:512], in0=o_[:], scalar=gsubc[:, 0:1],
                                                        in1=alp[:], op0=ALU.mult, op1=ALU.mult))
                tmp_free["o_"] = f_
                tmp_free["alp"] = f_
                sl = slice(qc * 512, (qc + 1) * 512)
                po = (hj % 2) * 64
                tko = ds_o[fb].issue(pool, [(oT_d[hj // 2, po:po + 64, sl], fin[fb][:, 0:512])], waits=[f_])
                fin_free[fb] = tko
                o_stores.append(tko)

            offs = (1, 7, 14, 17, 19, 25) if NT >= 24 else (1, 2, 4, 5, 6, 7)
            pending.append((i + offs[0], e1))
            pending.append((i + offs[1], e2))
            pending.append((i + offs[2], e3))
            if kind == "A":
                pending.append((i + offs[3], e4))
                pending.append((i + offs[4], e5))
                pending.append((i + offs[5], e6))

        n_units = len(units)
        PV_LAG = 3
        per_qc = 12 * NT
        for i in range(n_units + 32):
            if i < n_units:
                tk_qk = emit_qk(i)
                emit_exp(i, tk_qk)
            if PV_LAG <= i <= n_units + PV_LAG - 1:
                emit_pv(i - PV_LAG)
                if ((i - PV_LAG + 1) % per_qc) == 0:
                    qc_done = (i - PV_LAG + 1) // per_qc - 1
                    q_free[qc_done] = last_pv[0]
                    issue_q(qc_done + 2)
            still = []
            for (at, fn) in pending:
                if at <= i:
                    fn()
                else:
                    still.append((at, fn))
            pending[:] = still
        assert not pending
        res = K()
        res.done = [ds_o[0].last, ds_o[1].last, last_pv[0]]
        res.all_o = o_stores
        res.exit = pes
        return res

    del _ALL_DSEMS[:]
    _ALL_DSEMS.extend([ds_const] + ds_cast)
    try:
        P1 = token_phase(0)
    except _Stop as e_:
        print("STOP at", e_)
        for e in (pe, act, dve, pool):
            if e.n:
                sp.wait((e.name, e.sem, e.n))
        for d_ in _ALL_DSEMS:
            sp.wait(d_.last)
        return nc
    P1.exit.close()
    if 2 in phases:
        P2 = attention_phase()
        P2_done = P2.done
        P2.exit.close()
    else:
        P2_done = []
    if 3 in phases:
        P3 = token_phase(1)
        final_toks = P3.final
        P3.exit.close()
    else:
        final_toks = P1.final + list(P2_done)
    sp.wait(final_toks)
    es.close()
    return nc


def _t5_bucket(rel):
    nb = 16
    max_exact = 8
    ret = np.where(rel > 0, nb, 0)
    n = np.abs(rel)
    nf = np.maximum(n, 1).astype(np.float32)
    large = max_exact + (np.log(nf / np.float32(max_exact)) / np.float32(math.log(128 / max_exact))
                         * np.float32(nb - max_exact)).astype(np.int32)
    large = np.minimum(large, nb - 1)
    return ret + np.where(n < max_exact, n, large)


def _constants(S):
    ident = np.eye(128, dtype=np.float32).astype(ml_dtypes.bfloat16)
    u = np.arange(VL)
    bk = _t5_bucket(639 - u)
    ohm = np.zeros((32, VL), np.float32)
    ohm[bk, u] = 1.0
    rows = S // 64
    row = np.repeat(np.arange(rows), 64).astype(np.float32)
    col = np.tile(np.arange(64), rows).astype(np.float32)
    inv = (np.float32(10000.0) ** (-np.arange(0, 32, 2, dtype=np.float32) / np.float32(32))).astype(np.float32)
    ang = np.concatenate([row[:, None] * inv[None], col[:, None] * inv[None]], axis=-1).astype(np.float32)
    cos, sin = np.cos(ang).astype(np.float32), np.sin(ang).astype(np.float32)
    c2 = np.repeat(cos, 2, axis=1)
    s2 = np.stack([-sin, sin], axis=-1).reshape(S, 64)
    jm = np.ascontiguousarray(np.eye(128, dtype=np.float32)[::-1])
    return dict(ident=ident, ohm=ohm, jmat=jm, c2=np.ascontiguousarray(c2), s2=np.ascontiguousarray(s2))


_W1 = ("ffn1_w_gate", "ffn1_w_up", "ffn1_w_down", "ffn2_w_gate", "ffn2_w_up", "ffn2_w_down", "w_in", "w_out",
       "ffn1_norm", "ffn2_norm", "mix_norm", "lambda_q1", "lambda_k1", "lambda_q2", "lambda_k2",
       "diff_subln", "q_norm", "k_norm")


def _common_inputs(inputs, S):
    m = {}
    for k in _W1:
        a = np.asarray(inputs[k], dtype=np.float32)
        m[k] = np.ascontiguousarray(a[0])[None] if a[0].ndim == 1 else np.ascontiguousarray(a[0])
    m["final_norm"] = np.ascontiguousarray(np.asarray(inputs["final_norm"], np.float32))[None]
    m["rel_bias"] = np.ascontiguousarray(np.asarray(inputs["rel_bias"], np.float32))
    m.update(_constants(S))
    return m


def kernel(**inputs):
    x = np.asarray(inputs["x"], dtype=np.float32)
    B, S, _ = x.shape
    nc = build_program(S)
    common = _common_inputs(inputs, S)
    in_maps = []
    for b in range(B):
        m = dict(common)
        m["x"] = np.ascontiguousarray(x[b])
        in_maps.append(m)
    res = run_bass_kernel_spmd(nc, in_maps, core_ids=list(range(B)))
    return np.stack([np.asarray(r["out"], dtype=np.float32) for r in res.results], axis=0)
```
